# Optimizing a Trainium2 kernel written in Bass

```python
import jax, jax.numpy as jnp
from jax import lax
import numpy as np

D_MODEL = 4096
BATCH = 4
SEQ = 4096
DEPTH = 1

CHUNK = 64
Q_BLOCK = 128
PLE_DIM = 256
RET_HEADS = 8
RET_HEAD_DIM = D_MODEL // (2 * RET_HEADS)
RET_WIDTH = RET_HEADS * RET_HEAD_DIM
MLA_HEADS = 16
MLA_NOPE = 128
MLA_ROPE = 64
MLA_V = (D_MODEL - RET_WIDTH) // MLA_HEADS
Q_LORA = D_MODEL // 4
KV_LORA = D_MODEL // 8
D_FF = ((8 * D_MODEL // 3 + 255) // 256) * 256
CONV_WIDTH = 3
ROPE_BASE = 10000.0
EPS = 1e-6
IN_WIDTH = 4 * RET_WIDTH + Q_LORA + KV_LORA + MLA_ROPE
IN_SPLITS = (RET_WIDTH, 2 * RET_WIDTH, 3 * RET_WIDTH, 4 * RET_WIDTH,
             4 * RET_WIDTH + Q_LORA, 4 * RET_WIDTH + Q_LORA + KV_LORA)

kernel_name = "hybrid_retention_mla_convffn_ple"


def rms_norm(x, g):
    xf = x.astype(jnp.float32)
    y = xf * lax.rsqrt(jnp.mean(xf * xf, axis=-1, keepdims=True) + EPS)
    return (y * g.astype(jnp.float32)).astype(x.dtype)


def head_norm(o):
    of = o.astype(jnp.float32)
    mu = jnp.mean(of, axis=-1, keepdims=True)
    var = jnp.mean(jnp.square(of - mu), axis=-1, keepdims=True)
    return ((of - mu) * lax.rsqrt(var + EPS)).astype(o.dtype)


def rope_tables(seq, dim):
    inv = 1.0 / (ROPE_BASE ** (jnp.arange(0, dim, 2, dtype=jnp.float32) / dim))
    ang = jnp.arange(seq, dtype=jnp.float32)[:, None] * inv[None, :]
    return jnp.cos(ang), jnp.sin(ang)


def apply_rope(x, cos, sin):
    x1, x2 = jnp.split(x, 2, axis=-1)
    c = cos[None, :, None, :].astype(x.dtype)
    s = sin[None, :, None, :].astype(x.dtype)
    return jnp.concatenate([x1 * c - x2 * s, x2 * c + x1 * s], axis=-1)


def retention(q, k, v):
    B, S, H, dk = q.shape
    dv = v.shape[-1]
    nC = S // CHUNK
    dt = q.dtype
    log_g = jnp.log1p(-jnp.exp2(-5.0 - jnp.arange(H, dtype=jnp.float32)))
    idx = jnp.arange(CHUNK, dtype=jnp.float32)
    inner = jnp.exp(log_g[:, None, None] * jnp.abs(idx[:, None] - idx[None, :])).astype(dt)
    q_dec = jnp.exp(log_g[:, None] * (idx + 1.0))[..., None].astype(dt)
    k_dec = jnp.exp(log_g[:, None] * (CHUNK - 1.0 - idx))[..., None].astype(dt)
    s_dec = jnp.exp(log_g * CHUNK)[:, None, None].astype(dt)
    k = k * (dk ** -0.5)

    def to_chunks(t):
        return t.reshape(B, nC, CHUNK, H, t.shape[-1]).transpose(1, 0, 3, 2, 4)

    def step(state, qkv):
        qc, kc, vc = qkv
        scores = jnp.einsum('bhnd,bhmd->bhnm', qc, kc) * inner
        out = (jnp.einsum('bhnm,bhmv->bhnv', scores, vc)
               + jnp.einsum('bhnd,bhdv->bhnv', qc * q_dec, state))
        state = state * s_dec + jnp.einsum('bhmd,bhmv->bhdv', kc * k_dec, vc)
        return state, out

    s0 = jnp.zeros((B, H, dk, dv), dt)
    _, out = lax.scan(step, s0, (to_chunks(q), to_chunks(k), to_chunks(v)))
    return out.transpose(1, 0, 3, 2, 4).reshape(B, S, H, dv)


def mla_attention(q_nope, q_rope, k_nope, k_rope, v):
    B, S, H, _ = q_nope.shape
    nQ = S // Q_BLOCK
    scale = (MLA_NOPE + MLA_ROPE) ** -0.5
    key_chunk = jnp.arange(S) // CHUNK

    def blocks(t):
        return t.reshape(B, nQ, Q_BLOCK, *t.shape[2:]).swapaxes(0, 1)

    def one_block(args):
        qn, qr, b = args
        s = (jnp.einsum('bqhd,bkhd->bhqk', qn, k_nope)
             + jnp.einsum('bqhr,bkr->bhqk', qr, k_rope)).astype(jnp.float32) * scale
        q_chunk = (b * Q_BLOCK + jnp.arange(Q_BLOCK)) // CHUNK
        mask = key_chunk[None, :] <= q_chunk[:, None]
        s = jnp.where(mask[None, None], s, -jnp.inf)
        w = jax.nn.softmax(s, axis=-1).astype(v.dtype)
        return jnp.einsum('bhqk,bkhv->bqhv', w, v)

    out = lax.map(one_block, (blocks(q_nope), blocks(q_rope), jnp.arange(nQ)))
    return out.swapaxes(0, 1).reshape(B, S, H, v.shape[-1])


def causal_dwconv(h, w, b):
    S = h.shape[1]
    hp = jnp.pad(h, ((0, 0), (CONV_WIDTH - 1, 0), (0, 0)))
    out = b
    for j in range(CONV_WIDTH):
        out = out + hp[:, j:j + S, :] * w[j]
    return out


def setup_inputs(seed: int = 0) -> dict:
    key = jax.random.key(seed)
    ks = jax.random.split(key, 20)

    def nrm(k, shape, fan_in):
        return jax.random.normal(k, shape, jnp.float32) * (fan_in ** -0.5)

    def gain(k, shape):
        return 1.0 + 0.01 * jax.random.normal(k, shape, jnp.float32)

    L = DEPTH
    return {
        "x": jax.random.normal(ks[0], (BATCH, SEQ, D_MODEL), jnp.float32),
        "p": jax.random.normal(ks[1], (DEPTH, BATCH, SEQ, PLE_DIM), jnp.float32),
        "w_in": nrm(ks[2], (L, D_MODEL, IN_WIDTH), D_MODEL),
        "g_attn": gain(ks[3], (L, D_MODEL)),
        "g_q_lora": gain(ks[4], (L, Q_LORA)),
        "g_kv_lora": gain(ks[5], (L, KV_LORA)),
        "w_uq": nrm(ks[6], (L, Q_LORA, MLA_HEADS * (MLA_NOPE + MLA_ROPE)), Q_LORA),
        "w_ukv": nrm(ks[7], (L, KV_LORA, MLA_HEADS * (MLA_NOPE + MLA_V)), KV_LORA),
        "w_o": nrm(ks[8], (L, RET_WIDTH + MLA_HEADS * MLA_V, D_MODEL), D_MODEL),
        "g_ffn": gain(ks[9], (L, D_MODEL)),
        "w_ffn_gate": nrm(ks[10], (L, D_MODEL, D_FF), D_MODEL),
        "w_ffn_up": nrm(ks[11], (L, D_MODEL, D_FF), D_MODEL),
        "conv_w": nrm(ks[12], (L, CONV_WIDTH, D_FF), CONV_WIDTH),
        "conv_b": 0.01 * jax.random.normal(ks[13], (L, D_FF), jnp.float32),
        "w_ffn_down": nrm(ks[14], (L, D_FF, D_MODEL), D_FF),
        "g_ple": gain(ks[15], (L, D_MODEL)),
        "w_ple_gate": nrm(ks[16], (L, D_MODEL, D_MODEL), D_MODEL),
        "w_ple_proj": nrm(ks[17], (L, PLE_DIM, D_MODEL), PLE_DIM),
        "g_final": gain(ks[18], (D_MODEL,)),
    }


def reference(x, p, w_in, g_attn, g_q_lora, g_kv_lora, w_uq, w_ukv, w_o, g_ffn,
              w_ffn_gate, w_ffn_up, conv_w, conv_b, w_ffn_down, g_ple,
              w_ple_gate, w_ple_proj, g_final):
    B, S, _ = x.shape
    cos_r, sin_r = rope_tables(S, RET_HEAD_DIM)
    cos_m, sin_m = rope_tables(S, MLA_ROPE)
    h = x
    for i in range(DEPTH):
        hn = rms_norm(h, g_attn[i])
        proj = hn @ w_in[i]
        rq, rk, rv, rg, cq, ckv, kr = jnp.split(proj, IN_SPLITS, axis=-1)

        rq = apply_rope(rq.reshape(B, S, RET_HEADS, RET_HEAD_DIM), cos_r, sin_r)
        rk = apply_rope(rk.reshape(B, S, RET_HEADS, RET_HEAD_DIM), cos_r, sin_r)
        rv = rv.reshape(B, S, RET_HEADS, RET_HEAD_DIM)
        ro = head_norm(retention(rq, rk, rv)).reshape(B, S, RET_WIDTH)
        ro = jax.nn.silu(rg) * ro

        q = (rms_norm(cq, g_q_lora[i]) @ w_uq[i]).reshape(B, S, MLA_HEADS, MLA_NOPE + MLA_ROPE)
        q_nope = q[..., :MLA_NOPE]
        q_rope = apply_rope(q[..., MLA_NOPE:], cos_m, sin_m)
        kv = (rms_norm(ckv, g_kv_lora[i]) @ w_ukv[i]).reshape(B, S, MLA_HEADS, MLA_NOPE + MLA_V)
        k_nope = kv[..., :MLA_NOPE]
        v = kv[..., MLA_NOPE:]
        k_rope = apply_rope(kr[:, :, None, :], cos_m, sin_m)[:, :, 0, :]
        mo = mla_attention(q_nope, q_rope, k_nope, k_rope, v).reshape(B, S, MLA_HEADS * MLA_V)

        h = h + jnp.concatenate([ro, mo], axis=-1) @ w_o[i]

        hn = rms_norm(h, g_ffn[i])
        a = causal_dwconv(hn @ w_ffn_gate[i], conv_w[i], conv_b[i])
        h = h + (jax.nn.silu(a) * (hn @ w_ffn_up[i])) @ w_ffn_down[i]

        gate = jax.nn.sigmoid(rms_norm(h, g_ple[i]) @ w_ple_gate[i])
        h = h + gate * (p[i] @ w_ple_proj[i])
    return rms_norm(h, g_final)
```

```python
import numpy as np
from contextlib import ExitStack
import concourse.bass as bass
import concourse.mybir as mybir
from concourse.bass_utils import run_bass_kernel_spmd

F32 = mybir.dt.float32
BF16 = mybir.dt.bfloat16
AF = mybir.ActivationFunctionType
ALU = mybir.AluOpType

D = 4096
SEQ = 4096
HALF = 2048
INW = 9792
OFF_RQ, OFF_RK, OFF_RV, OFF_RG, OFF_CQ, OFF_CKV, OFF_KR = 0, 2048, 4096, 6144, 8192, 9216, 9728
QL, KVL, DFF, PLED = 1024, 512, 11008, 256
NFF = DFF // 128
EPS = 1e-6
SCALE = 192.0 ** -0.5
NEG = -30000.0
TILES = [(0, 512, "pre"), (512, 512, "pre"), (1024, 512, "pre"), (1536, 512, "halo"),
         (2048, 512, "own"), (2560, 512, "own"), (3072, 512, "own"), (3584, 512, "own")]
LIDX = {512: 0, 384: 1, 128: 2}
ENGS = ("pe", "act", "dve", "pool", "sp")


class Tok:
    __slots__ = ("sem", "val")

    def __init__(self, sem, val):
        self.sem = sem
        self.val = val


class Buf:
    def __init__(self, ap, arena, lo, hi):
        self.ap = ap
        self.reg = (arena, lo, hi)

    def __getitem__(self, k):
        return self.ap[k]

    def sub(self, lo, hi):
        return (self.reg[0], self.reg[1] + lo, self.reg[1] + hi)


def _regs(lst):
    out = []
    for x in lst:
        if x is None:
            continue
        if isinstance(x, Buf):
            out.append(x.reg)
        elif isinstance(x, list):
            out.extend(_regs(x))
        else:
            out.append(x)
    return out


class Prog:
    def __init__(self, nc, es):
        self.nc = nc
        self.es = es
        self.q = {e: [] for e in ENGS}
        self.psem = {e: es.enter_context(nc.semaphore("prog_" + e)) for e in ENGS}
        self.pcnt = {e: 0 for e in ENGS}
        self.seen = {e: {} for e in ENGS}
        self.dma_sems = {}
        self.dma_cnt = {}
        self.ar = {}
        self.nops = 0

    def _deps(self, reads, writes):
        toks = []
        for (a, lo, hi) in reads:
            d = self.ar.get(a)
            if d:
                for (l2, h2), ent in d.items():
                    if l2 < hi and lo < h2 and ent[0] is not None:
                        toks.append(ent[0])
        for (a, lo, hi) in writes:
            d = self.ar.get(a)
            if d:
                for (l2, h2), ent in d.items():
                    if l2 < hi and lo < h2:
                        if ent[0] is not None:
                            toks.append(ent[0])
                        toks.extend(ent[1].values())
        return toks

    def _commit(self, tok, reads, writes):
        for (a, lo, hi) in writes:
            d = self.ar.setdefault(a, {})
            for k in [k for k in d if lo <= k[0] and k[1] <= hi]:
                del d[k]
            d[(lo, hi)] = [tok, {}]
        for (a, lo, hi) in reads:
            d = self.ar.setdefault(a, {})
            ent = d.setdefault((lo, hi), [None, {}])
            ent[1][id(tok.sem)] = tok

    def _waits(self, eng, toks):
        out = []
        for t in toks:
            if t is None:
                continue
            if eng == "pe" and t.sem is self.psem["pe"]:
                continue
            k = id(t.sem)
            if self.seen[eng].get(k, -1) >= t.val:
                continue
            self.seen[eng][k] = t.val
            out.append((t.sem, t.val))
        return out

    def op(self, eng, fn, r=(), w=(), extra=()):
        reads, writes = _regs(r), _regs(w)
        wt = self._waits(eng, self._deps(reads, writes) + list(extra))
        self.pcnt[eng] += 1
        tok = Tok(self.psem[eng], self.pcnt[eng])
        self.q[eng].append((wt, [fn], (self.psem[eng], 1)))
        self._commit(tok, reads, writes)
        self.nops += 1
        return tok

    def group(self, eng, fns, r=(), w=(), extra=()):
        reads, writes = _regs(r), _regs(w)
        wt = self._waits(eng, self._deps(reads, writes) + list(extra))
        self.pcnt[eng] += 1
        tok = Tok(self.psem[eng], self.pcnt[eng])
        self.q[eng].append((wt, list(fns), (self.psem[eng], 1)))
        self._commit(tok, reads, writes)
        self.nops += len(fns)
        return tok

    def dma(self, eng, slot, fns, r=(), w=(), extra=()):
        if slot not in self.dma_sems:
            self.dma_sems[slot] = self.es.enter_context(self.nc.semaphore("dma_" + slot))
            self.dma_cnt[slot] = 0
        sem = self.dma_sems[slot]
        reads, writes = _regs(r), _regs(w)
        prev = [Tok(sem, self.dma_cnt[slot])] if self.dma_cnt[slot] else []
        wt = self._waits(eng, self._deps(reads, writes) + list(extra) + prev)
        tok = None
        for i, fn in enumerate(fns):
            self.dma_cnt[slot] += 16
            self.q[eng].append((wt if i == 0 else [], [fn], (sem, 16)))
        tok = Tok(sem, self.dma_cnt[slot])
        self._commit(tok, reads, writes)
        return tok

    def wait_only(self, eng, toks):
        wt = self._waits(eng, toks)
        if wt:
            self.q[eng].append((wt, [], None))

    def emit(self):
        nc = self.nc
        with nc.Block() as block:
            def run(name):
                def body(e):
                    for (wt, fns, inc) in self.q[name]:
                        for (sem, val) in wt:
                            e.wait_ge(sem, val)
                        ins = None
                        for fn in fns:
                            ins = fn(e)
                        if inc is not None and ins is not None:
                            ins.then_inc(inc[0], inc[1])
                return body
            block.tensor(run("pe"))
            block.scalar(run("act"))
            block.vector(run("dve"))
            block.gpsimd(run("pool"))
            block.sync(run("sp"))


class PsumPool:
    def __init__(self, nc, es):
        self.t = [es.enter_context(nc.psum_tensor(f"psb{i}", [128, 512], F32)) for i in range(8)]
        self.bufs = [Buf(self.t[i], "psum", i * 512, (i + 1) * 512) for i in range(8)]
        for i, b in enumerate(self.bufs):
            b.idx = i
            b.bf = self.t[i].bitcast(BF16)
        self.held = [False] * 8
        self.nxt = 0

    def alloc(self):
        for _ in range(8):
            i = self.nxt
            self.nxt = (self.nxt + 1) % 8
            if not self.held[i]:
                self.held[i] = True
                return self.bufs[i]
        raise RuntimeError("PSUM exhausted")

    def free(self, b):
        if isinstance(b, list):
            for x in b:
                self.free(x)
            return
        assert self.held[b.idx]
        self.held[b.idx] = False


STATS = {'mm': 0, 'mm_cols': 0, 'mm_cols_f32': 0, 'tr': 0, 'tr_f32': 0}


class _Stop(Exception):
    pass


def build_program(tiles=TILES, dbg=None, stop=None):
    nc = bass.Bass("TRN2", target_bir_lowering=False)

    def din(name, shape, dt=F32):
        return nc.dram_tensor(name, list(shape), dt, kind="ExternalInput").ap()

    xo = din("xo", [HALF, D]); xp = din("xp", [HALF, D]); pp = din("pp", [HALF, PLED])
    w_in = din("w_in", [D, INW]); w_uq = din("w_uq", [QL, 3072]); w_ukv = din("w_ukv", [KVL, 4096])
    w_o = din("w_o", [D, D]); w_fg = din("w_fg", [D, DFF]); w_fu = din("w_fu", [D, DFF])
    w_fd = din("w_fd", [DFF, D]); w_pg = din("w_pg", [D, D]); w_pp = din("w_pp", [PLED, D])
    c_cs = din("c_cs", [128, 4, SEQ])
    c_dmqd = din("c_dmqd", [8, 128, 2, 512])
    c_small = din("c_small", [128, 1024])
    c_gfin = din("c_gfin", [128, D])
    c_ident = din("c_ident", [128, 128])
    out = nc.dram_tensor("out", [HALF, D], F32, kind="ExternalOutput").ap()
    kc = nc.dram_tensor("kc_scratch", [16, 128, SEQ], BF16, kind="Internal").ap()
    vc = nc.dram_tensor("vc_scratch", [8, SEQ, 256], BF16, kind="Internal").ap()
    dbg_out = {}
    if dbg:
        for name, shape in dbg.items():
            dbg_out[name] = nc.dram_tensor("dbg_" + name, list(shape), F32, kind="ExternalOutput").ap()

    with ExitStack() as es:
        es.enter_context(nc.allow_low_precision("bf16 matmul operands, fp32 accumulation"))
        p = Prog(nc, es)
        ps = PsumPool(nc, es)

        def sb(name, shape, dt):
            return es.enter_context(nc.sbuf_tensor(name, list(shape), dt))

        A_t = sb("arenaA", [128, 16384], BF16)
        B_t = sb("arenaB", [128, 8192], F32)
        C_t = sb("arenaC", [128, 16384], F32)
        D_t = sb("arenaD", [128, 2048], F32)
        krT_t = sb("krT", [128, SEQ], BF16)
        S_t = sb("Sst", [128, 8 * 512], F32)
        small_t = sb("small", [128, 1024], F32)
        carry_t = sb("carry", [128, NFF * 2], F32)
        stat_t = sb("stat", [128, 64], F32)
        identF_t = sb("identF", [128, 128], F32)
        identB_t = sb("identB", [128, 128], BF16)
        onesF_t = sb("onesF", [128, 128], F32)
        onesB_t = sb("onesB", [128, 128], BF16)
        NW = 4
        wr_t = [sb(f"wring{i}", [128, 4096], BF16) for i in range(NW)]

        def vw(t, arena, lo, hi, dt=F32, pat=None, **kw):
            if arena == "A":
                ap = t[:, lo:hi]
            elif dt == F32:
                ap = t[:, lo:hi]
            else:
                ap = t.bitcast(BF16)[:, 2 * lo:2 * hi]
            if pat:
                ap = ap.rearrange(pat, **kw)
            return Buf(ap, arena, lo, hi)

        hnT = vw(A_t, "A", 0, 16384, BF16, "p (c t) -> p c t", c=32)
        concatT = vw(B_t, "B", 0, 8192, BF16, "p (c t) -> p c t", c=32)
        junk = vw(B_t, "B", 4096, 6144, BF16)
        Cblk = [vw(C_t, "C", b * 4096, (b + 1) * 4096) for b in range(4)]
        stgb = [vw(D_t, "D", i * 512, (i + 1) * 512, BF16) for i in range(4)]
        krT = Buf(krT_t, "krT", 0, SEQ)
        Sst = [Buf(S_t[:, h * 512:(h + 1) * 512], "S", h * 512, (h + 1) * 512) for h in range(8)]
        small = Buf(small_t, "small", 0, 1024)
        carry = Buf(carry_t[:, :].rearrange("p (c t) -> p c t", t=2), "carry", 0, NFF * 2)
        stat = Buf(stat_t, "stat", 0, 64)
        identF = Buf(identF_t, "identF", 0, 1); identB = Buf(identB_t, "identB", 0, 1)
        onesF = Buf(onesF_t, "onesF", 0, 1); onesB = Buf(onesB_t, "onesB", 0, 1)
        G_A, G_F, G_P, G_Q, G_KV, KB, KDEC, CW, CB, EPSC = 0, 32, 64, 96, 104, 108, 140, 236, 494, 580
        eps_ap = small_t[:, EPSC:EPSC + 1]

        qT = vw(C_t, "C", 0, 1024, BF16, "p (c t) -> p c t", c=4)
        qdT = vw(C_t, "C", 1024, 2048, BF16, "p (c t) -> p c t", c=4)
        kT = vw(C_t, "C", 2048, 3072, BF16, "p (c t) -> p c t", c=4)
        Vb = vw(C_t, "C", 3072, 4096, BF16, "p (b c) -> p b c", b=4)
        Vd = vw(C_t, "C", 4096, 5120, BF16, "p (b c) -> p b c", b=4)
        sgb = vw(C_t, "C", 5120, 7168, F32, "p (c t) -> p c t", c=4)
        T1 = vw(C_t, "C", 7168, 7680); T2 = vw(C_t, "C", 7680, 8192); R1 = vw(C_t, "C", 8192, 8704)
        ktok = vw(C_t, "C", 8704, 9216, BF16)
        scD = [vw(C_t, "C", 9216 + 256 * i, 9216 + 256 * (i + 1), BF16) for i in range(4)]
        o32 = vw(C_t, "C", 10240, 11264, F32, "p (c t) -> p c t", c=2)
        sq32 = vw(C_t, "C", 11264, 12288, F32, "p (c t) -> p c t", c=2)
        mu = vw(C_t, "C", 12288, 12800); rs = vw(C_t, "C", 12800, 13312); tm = vw(C_t, "C", 13312, 13824)
        Sbf = vw(C_t, "C", 13824, 14080, BF16)
        cs = vw(C_t, "C", 14336, 16384, F32, "p (k t) -> p k t", k=4)
        dmqd = vw(D_t, "D", 0, 2048, F32, "p (h k t) -> p h k t", h=2, k=2)
        cq32 = vw(C_t, "C", 0, 4096, F32, "p (c t) -> p c t", c=8)
        ckv32 = vw(C_t, "C", 4096, 6144, F32, "p (c t) -> p c t", c=4)
        sq16 = vw(C_t, "C", 6144, 8192, BF16, "p (c t) -> p c t", c=8)
        cqnT = vw(C_t, "C", 8192, 10240, BF16, "p (c t) -> p c t", c=8)
        ckvnT = vw(C_t, "C", 10240, 11264, BF16, "p (c t) -> p c t", c=4)
        rstdq = vw(C_t, "C", 11264, 11776); rstdkv = vw(C_t, "C", 11776, 12288)
        kra = vw(C_t, "C", 12288, 12800); krb = vw(C_t, "C", 12800, 13312)
        T1m = vw(C_t, "C", 13312, 13824); T2m = vw(C_t, "C", 13824, 14336)
        Kst = vw(C_t, "C", 0, 4096, BF16, "p (h t) -> p h t", h=16)
        Vst = vw(C_t, "C", 4096, 8192, BF16, "p (b c) -> p b c", b=4)
        kTp = [vw(C_t, "C", 0, 2048, BF16, "p (h t) -> p h t", h=2), vw(C_t, "C", 2048, 4096, BF16, "p (h t) -> p h t", h=2)]
        vvp = [vw(C_t, "C", 4096, 6144, BF16, "p (b c) -> p b c", c=256), vw(C_t, "C", 6144, 8192, BF16, "p (b c) -> p b c", c=256)]
        qn = vw(C_t, "C", 10240, 10752, BF16, "p (h t) -> p h t", h=2)
        qra = vw(C_t, "C", 10752, 11264); qrb = vw(C_t, "C", 11264, 11776)
        qr = vw(C_t, "C", 11776, 12032, BF16)
        PT = [vw(C_t, "C", 12032 + 256 * i, 12032 + 256 * (i + 1), BF16) for i in range(3)]
        rz = vw(C_t, "C", 12800, 13312)
        Wr = vw(D_t, "D", 0, 512, BF16, "p (k c) -> p k c", k=8)
        qrp1 = vw(D_t, "D", 512, 768, BF16)
        Qs = None
        PT = PT + [vw(D_t, "D", 1792, 2048, BF16), vw(C_t, "C", 14080, 14336, BF16)]
        qn2 = vw(D_t, "D", 768, 1280, BF16, "p (h t) -> p h t", h=2)
        qr2 = vw(D_t, "D", 1280, 1536, BF16)
        qrp1b = vw(D_t, "D", 1536, 1792, BF16)
        Qs = [(qn, qr, qrp1), (qn2, qr2, qrp1b)]
        actT = [vw(B_t, "B", 2048 * i, 2048 * (i + 1), BF16, "p (c t) -> p c t", c=8) for i in range(2)]
        NGX = 6
        gext = [vw(B_t, "B", 4096 + 514 * i, 4096 + 514 * (i + 1)) for i in range(NGX)]
        ft = [vw(D_t, "D", 512 * i, 512 * (i + 1)) for i in range(4)]
        gfin = vw(B_t, "B", 0, 4096)
        pbuf = vw(B_t, "B", 6144, 7168, F32, "p (b c) -> p b c", b=4)
        pT = vw(B_t, "B", 7168, 7680, BF16, "p (c t) -> p c t", c=2)
        ppo = [vw(D_t, "D", 512 * i, 512 * (i + 1)) for i in range(4)]
        sgm = [vw(B_t, "B", 7680 + 256 * 0, 7680 + 256 * 0 + 512)]
        sgm = [vw(C_t, "C", 0, 0)]
        sgt = [vw(B_t, "B", 4096, 4608), vw(B_t, "B", 4608, 5120)]

        def ACT(out_, in_, func, r, w, bias=None, scale=None, accum=None):
            kw = {}
            if bias is not None:
                kw["bias"] = bias
            if scale is not None:
                kw["scale"] = scale
            if accum is not None:
                kw["accum_out"] = accum
            return p.op("act", lambda e: e.activation(out=out_, in_=in_, func=func, **kw), r, w)

        def TT(out_, in0, in1, op, r, w, eng="dve"):
            return p.op(eng, lambda e: e.tensor_tensor(out=out_, in0=in0, in1=in1, op=op), r, w)

        def TS(out_, in0, s1, s2, op0, op1, r, w, eng="dve"):
            if op1 is None:
                return p.op(eng, lambda e: e.tensor_scalar(out=out_, in0=in0, scalar1=s1, scalar2=None, op0=op0), r, w)
            return p.op(eng, lambda e: e.tensor_scalar(out=out_, in0=in0, scalar1=s1, scalar2=s2, op0=op0, op1=op1), r, w)

        def STT(out_, in0, sc, in1, op0, op1, r, w, accum=None, eng="dve"):
            kw = {}
            if accum is not None:
                kw["accum_out"] = accum
            return p.op(eng, lambda e: e.scalar_tensor_tensor(out=out_, in0=in0, scalar=sc, in1=in1, op0=op0, op1=op1, **kw), r, w)

        def CP(out_, in_, r, w, eng="dve"):
            return p.op(eng, lambda e: e.tensor_copy(out=out_, in_=in_), r, w)

        def RCP(out_, in_, r, w):
            return p.op("dve", lambda e: e.reciprocal(out=out_, in_=in_), r, w)

        def MM(out_, lhsT, rhs, start, stop):
            STATS['mm'] += 1
            STATS['mm_cols'] += out_.shape[-1]
            if lhsT.dtype == F32:
                STATS['mm_cols_f32'] += out_.shape[-1]
            return lambda e: e.matmul(out_, lhsT, rhs, start=start, stop=stop)

        def TR(out_, in_, ident):
            STATS['tr'] += 1
            if in_.dtype == F32:
                STATS['tr_f32'] += 1
            return lambda e: e.transpose(out_, in_, ident)

        wstate = {"n": 0}

        def wload(parts, shape):
            i = wstate["n"] % NW
            wstate["n"] += 1
            t = wr_t[i]
            nk, ncol = shape
            view = t[:, 0:nk * ncol].rearrange("p (k c) -> p k c", k=nk)
            buf = Buf(view, "wring", i * 4096, (i + 1) * 4096)
            fns = []
            for (cs_, src) in parts:
                dst = view if cs_ is None else view[:, :, cs_[0]:cs_[1]]
                fns.append(lambda e, dst=dst, src=src: e.dma_start(out=dst, in_=src))
            p.dma("pool", f"w{i}", fns, r=[], w=[buf])
            return buf

        def wslab(w, k0, nk, c0, ncol):
            src = w.rearrange("(kc p) n -> p kc n", p=128)[:, k0:k0 + nk, c0:c0 + ncol]
            return wload([(None, src)], (nk, ncol))

        def fm_proj(w, c0, ncol, nkc, rhs_fn, rhs_regs, N, loader=None, extra=None):
            nch = ncol // 128
            banks = [ps.alloc() for _ in range(nch)]
            bx = ps.alloc() if extra else None
            for si, s0 in enumerate(range(0, nkc, 8)):
                sk = min(8, nkc - s0)
                slab = loader(s0, sk) if loader else wslab(w, s0, sk, c0, ncol)
                fns = []
                for j in range(nch):
                    for k in range(sk):
                        fns.append(MM(banks[j][:, 0:N], slab[:, k, j * 128:(j + 1) * 128], rhs_fn(s0 + k),
                                      s0 + k == 0, s0 + k == nkc - 1))
                if extra:
                    n2 = extra[2]
                    for j in range(nch):
                        col = (si * nch + j) * n2
                        for k in range(sk):
                            fns.append(MM(bx[:, col:col + n2], slab[:, k, j * 128:(j + 1) * 128], extra[0](s0 + k), k == 0, k == sk - 1))
                p.group("pe", fns, r=[slab] + rhs_regs + (extra[1] if extra else []), w=banks + ([bx] if extra else []))
            if extra:
                banks.append(bx)
            return banks

        def tm_proj(w, c0, ncol, k0, nkc, lhs_fn, lhs_regs, nb):
            banks = [ps.alloc() for _ in range(nb)]
            for s0 in range(0, nkc, 8):
                sk = min(8, nkc - s0)
                slab = wslab(w, k0 + s0, sk, c0, ncol)
                fns = []
                for b in range(nb):
                    for k in range(sk):
                        fns.append(MM(banks[b][:, 0:ncol], lhs_fn(s0 + k, b), slab[:, k, 0:ncol],
                                      s0 + k == 0, s0 + k == nkc - 1))
                p.group("pe", fns, r=[slab] + lhs_regs, w=banks)
            return banks

        p.dma("sp", "c0", [lambda e: e.dma_start(out=small_t[:, :], in_=c_small)], w=[small])
        p.dma("sp", "c1", [lambda e: e.dma_start(out=identF_t[:, :], in_=c_ident)], w=[identF])
        CP(identB_t[:, :], identF_t[:, :], [identF], [identB])
        p.op("dve", lambda e: e.memset(onesF_t[:, :], 1.0), [], [onesF])
        p.op("dve", lambda e: e.memset(onesB_t[:, :], 1.0), [], [onesB])
        p.op("dve", lambda e: e.memset(S_t[:, :], 0.0), [], Sst)
        p.op("dve", lambda e: e.memset(carry_t[:, :], 0.0), [], [carry])

        def norm_T(nb, gcol, blocks=None):
            it = 0
            for b in (blocks if blocks is not None else range(nb)):
                ACT(junk.ap, Cblk[b].ap, AF.Square, [Cblk[b]], [junk, stat.sub(b, b + 1)], accum=stat_t[:, b:b + 1])
                ACT(stat_t[:, 8 + b:9 + b], stat_t[:, b:b + 1], AF.Sqrt, [stat.sub(b, b + 1), small], [stat.sub(8 + b, 9 + b)],
                    bias=eps_ap, scale=1.0 / D)
                RCP(stat_t[:, 8 + b:9 + b], stat_t[:, 8 + b:9 + b], [stat.sub(8 + b, 9 + b)], [stat.sub(8 + b, 9 + b)])
                for q in range(4):
                    s = stgb[it % 4]
                    it += 1
                    ACT(s.ap, Cblk[b][:, q * 1024:(q + 1) * 1024], AF.Copy, [Cblk[b], stat.sub(8 + b, 9 + b)], [s],
                        scale=stat_t[:, 8 + b:9 + b])
                    bk = ps.alloc()
                    fns = [TR(bk.bf[:, i * 128:(i + 1) * 128], s[:, i * 128:(i + 1) * 128], identB_t[:, :]) for i in range(8)]
                    p.group("pe", fns, r=[s, identB], w=[bk])
                    c0 = q * 8
                    gb = small_t[:, gcol + c0:gcol + c0 + 8].unsqueeze(2).broadcast_to([128, 8, 128])
                    TT(hnT[:, c0:c0 + 8, b * 128:(b + 1) * 128], bk.bf[:, 0:1024].rearrange("p (c t) -> p c t", c=8), gb,
                       ALU.mult, [bk, small], [hnT])
                    ps.free(bk)

        def load_x(ltok0, L):
            nb = L // 128
            for b in range(nb):
                t0_ = ltok0 + b * 128
                src = (xp[t0_:t0_ + 128, :] if t0_ < HALF else xo[t0_ - HALF:t0_ - HALF + 128, :])
                p.dma("sp", f"xld{b}", [lambda e, src=src, b=b: e.dma_start(out=Cblk[b].ap, in_=src)], r=[], w=[Cblk[b]])

        def rope_pair(banks, N, outT, qd, toff):
            cosr, sinr = cs[:, 0, toff:toff + N], cs[:, 1, toff:toff + N]
            for hi in range(2):
                x1, x2 = banks[2 * hi], banks[2 * hi + 1]
                for half in range(2):
                    a, b_ = (x1, x2) if half == 0 else (x2, x1)
                    TT(T1[:, 0:N], a[:, 0:N], cosr, ALU.mult, [a, cs], [T1])
                    TT(T2[:, 0:N], b_[:, 0:N], sinr, ALU.mult, [b_, cs], [T2])
                    op = ALU.subtract if half == 0 else ALU.add
                    if qd is None:
                        TT(outT[:, 2 * hi + half, 0:N], T1[:, 0:N], T2[:, 0:N], op, [T1, T2], [outT])
                    else:
                        TT(R1[:, 0:N], T1[:, 0:N], T2[:, 0:N], op, [T1, T2], [R1])
                        ACT(outT[:, 2 * hi + half, 0:N], R1[:, 0:N], AF.Copy, [R1], [outT])
                        TT(qd[:, 2 * hi + half, 0:N], R1[:, 0:N], dmqd[:, hi, 1, 0:N], ALU.mult, [R1, dmqd], [qd])

        def hn_rhs(N0, N1):
            return lambda k: hnT[:, k, N0:N1]

        hhalo_t = sb("hhalo", [128, 64], BF16)
        hhalo = Buf(hhalo_t[:, :].rearrange("p (c t) -> p c t", t=2), "hhalo", 0, 64)

        def chk(k):
            if stop == k:
                raise _Stop()

        first_own = True
        for (ltok0, L, mode) in tiles:
          try:
            nb = L // 128
            hasq = mode != "pre"
            q0 = 384 if mode == "halo" else 0
            Lq = (L - q0) if hasq else 0
            qb0 = q0 // 128
            nbq = Lq // 128
            load_x(ltok0, L)
            norm_T(nb, G_A)
            p.dma("sp", "csld", [lambda e, l0=ltok0, L=L: e.dma_start(out=cs[:, :, 0:L], in_=c_cs[:, :, l0:l0 + L])], r=[], w=[cs])

            chk(1)
            for hp in range(4):
                h0 = 2 * hp
                if hasq:
                    p.dma("sp", "dmqd", [lambda e, h0=h0: e.dma_start(out=dmqd.ap, in_=c_dmqd[h0:h0 + 2].rearrange("h p k t -> p h k t"))],
                          r=[], w=[dmqd])
                bk_ = fm_proj(w_in, OFF_RK + 512 * hp, 512, 32, hn_rhs(0, L), [hnT], L)
                rope_pair(bk_, L, kT, None, 0)
                ps.free(bk_)
                if hasq:
                    bq_ = fm_proj(w_in, OFF_RQ + 512 * hp, 512, 32, hn_rhs(q0, L), [hnT], Lq)
                    rope_pair(bq_, Lq, qT, qdT, q0)
                    ps.free(bq_)
                bv_ = tm_proj(w_in, OFF_RV + 512 * hp, 512, 0, 32, lambda k, b: hnT[:, k, b * 128:(b + 1) * 128], [hnT], nb)
                for b in range(nb):
                    if hasq and b >= qb0:
                        ACT(Vb[:, b - qb0, :], bv_[b][:, :], AF.Copy, [bv_[b]], [Vb])
                    if b < qb0:
                        lidx, bb = LIDX[q0], b
                    else:
                        lidx, bb = LIDX[L - q0], b - qb0
                    for hi in range(2):
                        col = KDEC + lidx * 32 + (h0 + hi) * 4 + bb
                        ACT(Vd[:, b, hi * 256:(hi + 1) * 256], bv_[b][:, hi * 256:(hi + 1) * 256], AF.Copy, [bv_[b], small], [Vd],
                            scale=small_t[:, col:col + 1])
                ps.free(bv_)
                if hasq:
                    bg_ = fm_proj(w_in, OFF_RG + 512 * hp, 512, 32, hn_rhs(q0, L), [hnT], Lq)
                    for j in range(4):
                        ACT(sgb[:, j, 0:Lq], bg_[j][:, 0:Lq], AF.Silu, [bg_[j]], [sgb])
                    ps.free(bg_)
                for hi in range(2):
                    h = h0 + hi
                    bk = ps.alloc()
                    fns = [TR(bk.bf[:, b * 256 + dc * 128:b * 256 + (dc + 1) * 128], kT[:, 2 * hi + dc, b * 128:(b + 1) * 128], identB_t[:, :])
                           for b in range(nb) for dc in range(2)]
                    p.group("pe", fns, r=[kT, identB], w=[bk])
                    ACT(ktok[:, 0:nb * 256], bk.bf[:, 0:nb * 256], AF.Copy, [bk], [ktok])
                    ps.free(bk)

                    def state_update(b_lo, b_hi, ntok, h=h, hi=hi):
                        bu = ps.alloc()
                        fns = []
                        for dc in range(2):
                            for b in range(b_lo, b_hi):
                                fns.append(MM(bu[:, dc * 256:(dc + 1) * 256], ktok[:, b * 256 + dc * 128:b * 256 + (dc + 1) * 128],
                                              Vd[:, b, hi * 256:(hi + 1) * 256], b == b_lo, b == b_hi - 1))
                        p.group("pe", fns, r=[ktok, Vd], w=[bu])
                        sdec = float(np.exp(np.log1p(-2.0 ** (-5.0 - h)) * ntok))
                        STT(Sst[h].ap, Sst[h].ap, sdec, bu[:, :], ALU.mult, ALU.add, [Sst[h], bu], [Sst[h]])
                        ps.free(bu)

                    if qb0 > 0:
                        state_update(0, qb0, q0)
                    if hasq:
                        ACT(Sbf.ap, Sst[h].ap, AF.Copy, [Sst[h]], [Sbf])
                        for kb in range(nbq):
                            N = Lq - 128 * kb
                            bs = ps.alloc()
                            fns = [MM(bs[:, 0:N], kT[:, 2 * hi + dc, q0 + kb * 128:q0 + (kb + 1) * 128], qT[:, 2 * hi + dc, 128 * kb:Lq], dc == 0, dc == 1)
                                   for dc in range(2)]
                            p.group("pe", fns, r=[kT, qT], w=[bs])
                            TT(scD[kb][:, 0:N], bs[:, 0:N], dmqd[:, hi, 0, 0:N], ALU.mult, [bs, dmqd], [scD[kb]])
                            ps.free(bs)
                        for vcx in range(2):
                            bo = ps.alloc()
                            fns = [MM(bo[:, 0:Lq], Sbf[:, dc * 256 + vcx * 128:dc * 256 + (vcx + 1) * 128], qdT[:, 2 * hi + dc, 0:Lq], dc == 0, False)
                                   for dc in range(2)]
                            for kb in range(nbq):
                                N = Lq - 128 * kb
                                fns.append(MM(bo[:, 128 * kb:Lq], Vb[:, kb, hi * 256 + vcx * 128:hi * 256 + (vcx + 1) * 128], scD[kb][:, 0:N],
                                              False, kb == nbq - 1))
                            p.group("pe", fns, r=[Sbf, qdT, Vb] + scD[0:nbq], w=[bo])
                            ACT(o32[:, vcx, 0:Lq], bo[:, 0:Lq], AF.Copy, [bo], [o32])
                            ACT(sq32[:, vcx, 0:Lq], bo[:, 0:Lq], AF.Square, [bo], [sq32])
                            ps.free(bo)
                        bm = ps.alloc()
                        p.group("pe", [MM(bm[:, 0:Lq], onesF_t[:, :], o32[:, v, 0:Lq], v == 0, v == 1) for v in range(2)], r=[onesF, o32], w=[bm])
                        bq2 = ps.alloc()
                        p.group("pe", [MM(bq2[:, 0:Lq], onesF_t[:, :], sq32[:, v, 0:Lq], v == 0, v == 1) for v in range(2)], r=[onesF, sq32], w=[bq2])
                        ACT(mu[:, 0:Lq], bm[:, 0:Lq], AF.Copy, [bm], [mu], scale=1.0 / 256)
                        ps.free(bm)
                        TT(tm[:, 0:Lq], mu[:, 0:Lq], mu[:, 0:Lq], ALU.mult, [mu], [tm])
                        STT(rs[:, 0:Lq], bq2[:, 0:Lq], 1.0 / 256, tm[:, 0:Lq], ALU.mult, ALU.subtract, [bq2, tm], [rs])
                        ps.free(bq2)
                        ACT(rs[:, 0:Lq], rs[:, 0:Lq], AF.Sqrt, [rs, small], [rs], bias=eps_ap, scale=1.0)
                        RCP(rs[:, 0:Lq], rs[:, 0:Lq], [rs], [rs])
                        for vcx in range(2):
                            TT(tm[:, 0:Lq], o32[:, vcx, 0:Lq], mu[:, 0:Lq], ALU.subtract, [o32, mu], [tm])
                            TT(tm[:, 0:Lq], tm[:, 0:Lq], rs[:, 0:Lq], ALU.mult, [tm, rs], [tm])
                            TT(concatT[:, 2 * h + vcx, 0:Lq], tm[:, 0:Lq], sgb[:, 2 * hi + vcx, 0:Lq], ALU.mult, [tm, sgb], [concatT])
                    state_update(qb0, nb, L - q0)

            chk(2)
            bc_ = fm_proj(w_in, OFF_CKV, 512, 32, hn_rhs(0, L), [hnT], L)
            for j in range(4):
                ACT(ckv32[:, j, 0:L], bc_[j][:, 0:L], AF.Copy, [bc_[j]], [ckv32])
                ACT(sq16[:, j, 0:L], bc_[j][:, 0:L], AF.Square, [bc_[j]], [sq16])
            ps.free(bc_)
            bst = ps.alloc()
            p.group("pe", [MM(bst[:, 0:L], onesB_t[:, :], sq16[:, j, 0:L], j == 0, j == 3) for j in range(4)], r=[onesB, sq16], w=[bst])
            ACT(rstdkv[:, 0:L], bst[:, 0:L], AF.Sqrt, [bst, small], [rstdkv], bias=eps_ap, scale=1.0 / KVL)
            ps.free(bst)
            RCP(rstdkv[:, 0:L], rstdkv[:, 0:L], [rstdkv], [rstdkv])
            for j in range(4):
                STT(ckvnT[:, j, 0:L], ckv32[:, j, 0:L], small_t[:, G_KV + j:G_KV + j + 1], rstdkv[:, 0:L], ALU.mult, ALU.mult,
                    [ckv32, small, rstdkv], [ckvnT])
            w_kr = w_in.rearrange("(kc p) n -> p kc n", p=128)

            def kr_loader(s0, sk):
                src = w_kr[:, s0:s0 + sk, OFF_KR:OFF_KR + 64]
                return wload([((0, 64), src), ((64, 128), src)], (sk, 128))

            bkr = fm_proj(None, 0, 128, 32, hn_rhs(0, L), [hnT], L, loader=kr_loader)
            ACT(kra[:, 0:L], bkr[0][:, 0:L], AF.Copy, [bkr[0]], [kra])
            ps.free(bkr)
            for (d0, s0_) in ((0, 32), (32, 0), (64, 96), (96, 64)):
                CP(krb.ap[d0:d0 + 32, 0:L], kra.ap[s0_:s0_ + 32, 0:L], [kra], [krb])
            TT(T1m[:, 0:L], kra[:, 0:L], cs[:, 2, 0:L], ALU.mult, [kra, cs], [T1m])
            TT(T2m[:, 0:L], krb[:, 0:L], cs[:, 3, 0:L], ALU.mult, [krb, cs], [T2m])
            TT(krT_t[:, ltok0:ltok0 + L], T1m[:, 0:L], T2m[:, 0:L], ALU.add, [T1m, T2m], [krT.sub(ltok0, ltok0 + L)])
            if hasq:
                for g in range(2):
                    bcq = fm_proj(w_in, OFF_CQ + 512 * g, 512, 32, hn_rhs(q0, L), [hnT], Lq)
                    for j in range(4):
                        ACT(cq32[:, 4 * g + j, 0:Lq], bcq[j][:, 0:Lq], AF.Copy, [bcq[j]], [cq32])
                        ACT(sq16[:, 4 * g + j, 0:Lq], bcq[j][:, 0:Lq], AF.Square, [bcq[j]], [sq16])
                    ps.free(bcq)
                bst = ps.alloc()
                p.group("pe", [MM(bst[:, 0:Lq], onesB_t[:, :], sq16[:, j, 0:Lq], j == 0, j == 7) for j in range(8)], r=[onesB, sq16], w=[bst])
                ACT(rstdq[:, 0:Lq], bst[:, 0:Lq], AF.Sqrt, [bst, small], [rstdq], bias=eps_ap, scale=1.0 / QL)
                ps.free(bst)
                RCP(rstdq[:, 0:Lq], rstdq[:, 0:Lq], [rstdq], [rstdq])
                for j in range(8):
                    STT(cqnT[:, j, 0:Lq], cq32[:, j, 0:Lq], small_t[:, G_Q + j:G_Q + j + 1], rstdq[:, 0:Lq], ALU.mult, ALU.mult,
                        [cq32, small, rstdq], [cqnT])

            chk(3)
            for s in range(4):
                slab = wslab(w_ukv, 0, 4, 1024 * s, 1024)
                for hh in range(4):
                    bk = ps.alloc()
                    p.group("pe", [MM(bk[:, 0:L], slab[:, k, hh * 256:hh * 256 + 128], ckvnT[:, k, 0:L], k == 0, k == 3) for k in range(4)],
                            r=[slab, ckvnT], w=[bk])
                    ACT(Kst[:, 4 * s + hh, 0:L], bk[:, 0:L], AF.Copy, [bk], [Kst])
                    ps.free(bk)
                for b in range(nb):
                    bv = ps.alloc()
                    fns = []
                    for hh in range(4):
                        for k in range(4):
                            fns.append(MM(bv[:, hh * 128:(hh + 1) * 128], ckvnT[:, k, b * 128:(b + 1) * 128],
                                          slab[:, k, hh * 256 + 128:hh * 256 + 256], k == 0, k == 3))
                    p.group("pe", fns, r=[slab, ckvnT], w=[bv])
                    CP(Vst[:, b, 512 * s:512 * (s + 1)], bv[:, :], [bv], [Vst])
                    ps.free(bv)
            kreg, vreg = ("kc", 0, 1), ("vc", 0, 1)
            p.dma("sp", "kvst", [lambda e, l0=ltok0, L=L: e.dma_start(out=kc[:, :, l0:l0 + L].rearrange("h d t -> d h t"), in_=Kst[:, :, 0:L])],
                  r=[Kst], w=[kreg])
            fns = []
            for b in range(nb):
                fns.append(lambda e, b=b, l0=ltok0: e.dma_start(
                    out=vc[:, l0 + b * 128:l0 + (b + 1) * 128, :].rearrange("q p c -> p q c"),
                    in_=Vst[:, b, :].rearrange("p (q c) -> p q c", q=8)))
            p.dma("sp", "kvst2", fns, r=[Vst], w=[vreg])

            chk(4)
            if not hasq:
                continue

            nkb = (ltok0 + L) // 128
            kh = nkb // 2
            dblk0 = (ltok0 + q0) // 128
            pti = 0

            def qproj(hp_):
                qn_b, qr_b, qrp_b = Qs[hp_ % 2]
                slab = wslab(w_uq, 0, 8, 384 * hp_, 384)
                CP(Wr[:, :, 0:64], slab[:, :, 128:192], [slab], [Wr])
                CP(Wr[:, :, 64:128], slab[:, :, 320:384], [slab], [Wr])
                for hi in range(2):
                    bqn = ps.alloc()
                    p.group("pe", [MM(bqn[:, 0:Lq], slab[:, k, hi * 192:hi * 192 + 128], cqnT[:, k, 0:Lq], k == 0, k == 7) for k in range(8)],
                            r=[slab, cqnT], w=[bqn])
                    ACT(qn_b[:, hi, 0:Lq], bqn[:, 0:Lq], AF.Copy, [bqn], [qn_b])
                    ps.free(bqn)
                bqr = ps.alloc()
                p.group("pe", [MM(bqr[:, 0:Lq], Wr[:, k, :], cqnT[:, k, 0:Lq], k == 0, k == 7) for k in range(8)], r=[Wr, cqnT], w=[bqr])
                ACT(qra[:, 0:Lq], bqr[:, 0:Lq], AF.Copy, [bqr], [qra])
                ps.free(bqr)
                for (d0, s0_) in ((0, 32), (32, 0), (64, 96), (96, 64)):
                    CP(qrb.ap[d0:d0 + 32, 0:Lq], qra.ap[s0_:s0_ + 32, 0:Lq], [qra], [qrb])
                TT(T1m[:, 0:Lq], qra[:, 0:Lq], cs[:, 2, q0:q0 + Lq], ALU.mult, [qra, cs], [T1m])
                TT(T2m[:, 0:Lq], qrb[:, 0:Lq], cs[:, 3, q0:q0 + Lq], ALU.mult, [qrb, cs], [T2m])
                TT(qr_b[:, 0:Lq], T1m[:, 0:Lq], T2m[:, 0:Lq], ALU.add, [T1m, T2m], [qr_b])
                p.op("dve", lambda e: e.memset(qrp_b.ap[0:64, 0:Lq], 0.0), [], [qrp_b])
                CP(qrp_b.ap[64:128, 0:Lq], qr_b.ap[64:128, 0:Lq], [qr_b], [qrp_b])
                p.op("dve", lambda e: e.memset(qr_b.ap[64:128, 0:Lq], 0.0), [qrp_b], [qr_b])

            qproj(0)
            for hp in range(8):
                for half, (b0, b1) in enumerate(((0, kh), (kh, nkb))):
                    nk_ = (b1 - b0) * 128
                    p.dma("sp", f"kld{half}", [lambda e, hp=hp, b0=b0, nk_=nk_, half=half: e.dma_start(
                        out=kTp[half][:, :, 0:nk_], in_=kc[2 * hp:2 * hp + 2, :, b0 * 128:b0 * 128 + nk_].rearrange("h d t -> d h t"))],
                        r=[kreg], w=[kTp[half]])
                    p.dma("sp", f"vld{half}", [lambda e, hp=hp, b0=b0, b1=b1, half=half: e.dma_start(
                        out=vvp[half][:, 0:b1 - b0, :], in_=vc[hp, b0 * 128:b1 * 128, :].rearrange("(b p) c -> p b c", p=128))],
                        r=[vreg], w=[vvp[half]])
                qn_c, qr_c, qrp_c = Qs[hp % 2]
                if hp < 7:
                    qproj(hp + 1)
                bo2 = [ps.alloc(), ps.alloc()]
                bz2 = [ps.alloc(), ps.alloc()]

                def scores(kb, hi):
                    nonlocal pti
                    half = 0 if kb < kh else 1
                    kl = kb - (0 if half == 0 else kh)
                    diag = kb >= dblk0
                    qlo = 128 * (kb - dblk0) if diag else 0
                    N = Lq - qlo
                    bs = ps.alloc()
                    fns = [MM(bs[:, 0:N], kTp[half][:, hi, kl * 128:(kl + 1) * 128], qn_c[:, hi, qlo:Lq], True, False),
                           MM(bs[:, 0:N], krT_t[:, kb * 128:(kb + 1) * 128], (qr_c if hi == 0 else qrp_c)[:, qlo:Lq], False, True)]
                    p.group("pe", fns, r=[kTp[half], qn_c, krT.sub(kb * 128, (kb + 1) * 128), qr_c, qrp_c], w=[bs])
                    pt = PT[pti % len(PT)]
                    pti += 1
                    if diag:
                        ACT(pt[:, 0:N], bs[:, 0:N], AF.Exp, [bs], [pt], scale=SCALE)
                        p.op("dve", lambda e, pt=pt: e.memset(pt.ap[64:128, 0:64], 0.0), [], [pt])
                    else:
                        ACT(pt[:, 0:N], bs[:, 0:N], AF.Exp, [bs, small], [pt], scale=SCALE, bias=small_t[:, KB + kb:KB + kb + 1])
                    ps.free(bs)
                    return (pt, N, qlo, half, kl)

                nxt = [scores(0, 0), scores(0, 1)]
                for kb in range(nkb):
                    cur = nxt
                    if kb + 1 < nkb:
                        nxt = [scores(kb + 1, 0), scores(kb + 1, 1)]
                    for hi in range(2):
                        pt, N, qlo, half, kl = cur[hi]
                        p.group("pe", [MM(bo2[hi][:, qlo:Lq], vvp[half][:, kl, hi * 128:(hi + 1) * 128], pt[:, 0:N], kb == 0, kb == nkb - 1),
                                       MM(bz2[hi][:, qlo:Lq], onesB_t[:, :], pt[:, 0:N], kb == 0, kb == nkb - 1)],
                                r=[vvp[half], pt, onesB], w=[bo2[hi], bz2[hi]])
                rzb = [rz, T2m]
                for hi in range(2):
                    ACT(rzb[hi][:, 0:Lq], bz2[hi][:, 0:Lq], AF.Ln, [bz2[hi]], [rzb[hi]])
                    ps.free(bz2[hi])
                for hi in range(2):
                    ACT(rzb[hi][:, 0:Lq], rzb[hi][:, 0:Lq], AF.Exp, [rzb[hi]], [rzb[hi]], scale=-1.0)
                for hi in range(2):
                    h = 2 * hp + hi
                    TT(concatT[:, 16 + h, 0:Lq], bo2[hi][:, 0:Lq], rzb[hi][:, 0:Lq], ALU.mult, [bo2[hi], rzb[hi]], [concatT])
                    ps.free(bo2[hi])

            chk(5)
            load_x(ltok0 + q0, Lq)
            for n in range(8):
                bw = tm_proj(w_o, 512 * n, 512, 0, 32, lambda k, b: concatT[:, k, b * 128:(b + 1) * 128], [concatT], nbq)
                for b in range(nbq):
                    reg = Cblk[b].sub(512 * n, 512 * (n + 1))
                    TT(Cblk[b][:, 512 * n:512 * (n + 1)], bw[b][:, :], Cblk[b][:, 512 * n:512 * (n + 1)], ALU.add, [bw[b], reg], [reg])
                ps.free(bw)
            if dbg and "h1" in dbg and mode == "own" and ltok0 == HALF:
                p.dma("sp", "dbg", [lambda e: e.dma_start(out=dbg_out["h1"], in_=Cblk[0].ap)], r=[Cblk[0]], w=[])

            chk(6)
            norm_T(nbq, G_F)
            if mode == "halo":
                CP(hhalo.ap, hnT[:, :, Lq - 2:Lq], [hnT], [hhalo])
                continue
            use_halo = first_own
            first_own = False
            gxs = {"i": 0}

            def GU(sgi):
                sg0 = sgi * 8
                nsg = min(8, NFF - sg0)
                aT = actT[sgi % 2]
                for g0 in range(sg0, sg0 + nsg, 4):
                    ng = min(4, sg0 + nsg - g0)
                    extra = (lambda k: hhalo[:, k, :], [hhalo], 2) if use_halo else None
                    gb_ = fm_proj(w_fg, g0 * 128, ng * 128, 32, hn_rhs(0, L), [hnT], L, extra=extra)
                    bx = gb_.pop() if use_halo else None
                    ges = []
                    for j in range(ng):
                        c = g0 + j
                        ge = gext[gxs["i"] % NGX]
                        gxs["i"] += 1
                        ges.append(ge)
                        if use_halo:
                            tb = stat.sub(16 + 8 * (j % 2), 24 + 8 * (j % 2))
                            t0c = 16 + 8 * (j % 2)
                            for si in range(4):
                                col = (si * ng + j) * 2
                                ACT(stat_t[:, t0c + 2 * si:t0c + 2 * si + 2], bx[:, col:col + 2], AF.Copy, [bx], [tb])
                            TT(ge[:, 0:2], stat_t[:, t0c:t0c + 2], stat_t[:, t0c + 2:t0c + 4], ALU.add, [tb], [ge])
                            TT(ge[:, 0:2], ge[:, 0:2], stat_t[:, t0c + 4:t0c + 6], ALU.add, [tb, ge], [ge])
                            TT(ge[:, 0:2], ge[:, 0:2], stat_t[:, t0c + 6:t0c + 8], ALU.add, [tb, ge], [ge])
                        else:
                            ACT(ge[:, 0:2], carry[:, c, :], AF.Copy, [carry], [ge])
                        ACT(ge[:, 2:2 + L], gb_[j][:, 0:L], AF.Copy, [gb_[j]], [ge])
                        ACT(carry[:, c, :], ge[:, L:L + 2], AF.Copy, [ge], [carry])
                    ps.free(gb_)
                    if bx is not None:
                        ps.free(bx)
                    for j in range(ng):
                        c = g0 + j
                        ge = ges[j]
                        t1 = ft[j]
                        cw = lambda i, c=c: small_t[:, CW + 3 * c + i:CW + 3 * c + i + 1]
                        TS(t1[:, 0:L], ge[:, 0:L], cw(0), small_t[:, CB + c:CB + c + 1], ALU.mult, ALU.add, [ge, small], [t1])
                        STT(t1[:, 0:L], ge[:, 1:L + 1], cw(1), t1[:, 0:L], ALU.mult, ALU.add, [ge, small, t1], [t1])
                        STT(t1[:, 0:L], ge[:, 2:L + 2], cw(2), t1[:, 0:L], ALU.mult, ALU.add, [ge, small, t1], [t1])
                        ACT(t1[:, 0:L], t1[:, 0:L], AF.Silu, [t1], [t1])
                    ub_ = fm_proj(w_fu, g0 * 128, ng * 128, 32, hn_rhs(0, L), [hnT], L)
                    for j in range(ng):
                        c = g0 + j
                        TT(aT[:, c - sg0, 0:L], ft[j][:, 0:L], ub_[j][:, 0:L], ALU.mult, [ft[j], ub_[j]], [aT])
                    ps.free(ub_)

            def DN(sgi):
                sg0 = sgi * 8
                nsg = min(8, NFF - sg0)
                aT = actT[sgi % 2]
                for n in range(8):
                    bd = tm_proj(w_fd, 512 * n, 512, sg0, nsg, lambda k, b, aT=aT: aT[:, k, b * 128:(b + 1) * 128], [aT], nb)
                    for b in range(nb):
                        reg = Cblk[b].sub(512 * n, 512 * (n + 1))
                        TT(Cblk[b][:, 512 * n:512 * (n + 1)], bd[b][:, :], Cblk[b][:, 512 * n:512 * (n + 1)], ALU.add, [bd[b], reg], [reg])
                    ps.free(bd)

            nsgs = (NFF + 7) // 8
            GU(0)
            for sgi in range(1, nsgs):
                GU(sgi)
                DN(sgi - 1)
            DN(nsgs - 1)
            if dbg and "h2" in dbg and ltok0 == HALF:
                p.dma("sp", "dbg", [lambda e: e.dma_start(out=dbg_out["h2"], in_=Cblk[0].ap)], r=[Cblk[0]], w=[])

            chk(7)
            t0 = ltok0 - HALF
            p.dma("sp", "pld", [lambda e, t0=t0: e.dma_start(out=pbuf.ap, in_=pp[t0:t0 + L, :].rearrange("(b p) c -> p b c", p=128))],
                  r=[], w=[pbuf])
            for kc_ in range(2):
                bk = ps.alloc()
                p.group("pe", [TR(bk[:, b * 128:(b + 1) * 128], pbuf[:, b, kc_ * 128:(kc_ + 1) * 128], identF_t[:, :]) for b in range(nb)],
                        r=[pbuf, identF], w=[bk])
                ACT(pT[:, kc_, 0:L], bk[:, 0:L], AF.Copy, [bk], [pT])
                ps.free(bk)
            norm_T(nb, G_P)
            p.dma("sp", "gfin", [lambda e: e.dma_start(out=gfin.ap, in_=c_gfin)], r=[], w=[gfin])
            for n in range(8):
                bp = tm_proj(w_pp, 512 * n, 512, 0, 2, lambda k, b: pT[:, k, b * 128:(b + 1) * 128], [pT], nb)
                for b in range(nb):
                    ACT(ppo[b].ap, bp[b][:, :], AF.Copy, [bp[b]], [ppo[b]])
                ps.free(bp)
                bg = tm_proj(w_pg, 512 * n, 512, 0, 32, lambda k, b: hnT[:, k, b * 128:(b + 1) * 128], [hnT], nb)
                for b in range(nb):
                    s_ = sgt[b % 2]
                    ACT(s_.ap, bg[b][:, :], AF.Sigmoid, [bg[b]], [s_])
                    TT(s_.ap, s_.ap, ppo[b].ap, ALU.mult, [s_, ppo[b]], [s_])
                    reg = Cblk[b].sub(512 * n, 512 * (n + 1))
                    TT(Cblk[b][:, 512 * n:512 * (n + 1)], s_.ap, Cblk[b][:, 512 * n:512 * (n + 1)], ALU.add, [s_, reg], [reg])
                ps.free(bg)

            chk(8)
            for b in range(nb):
                ACT(junk.ap, Cblk[b].ap, AF.Square, [Cblk[b]], [junk, stat.sub(b, b + 1)], accum=stat_t[:, b:b + 1])
                ACT(stat_t[:, 8 + b:9 + b], stat_t[:, b:b + 1], AF.Sqrt, [stat.sub(b, b + 1), small], [stat.sub(8 + b, 9 + b)],
                    bias=eps_ap, scale=1.0 / D)
                RCP(stat_t[:, 8 + b:9 + b], stat_t[:, 8 + b:9 + b], [stat.sub(8 + b, 9 + b)], [stat.sub(8 + b, 9 + b)])
                STT(Cblk[b].ap, Cblk[b].ap, stat_t[:, 8 + b:9 + b], gfin.ap, ALU.mult, ALU.mult, [Cblk[b], stat.sub(8 + b, 9 + b), gfin], [Cblk[b]])
                p.dma("sp", f"ost{b}", [lambda e, t0=t0, b=b: e.dma_start(out=out[t0 + 128 * b:t0 + 128 * (b + 1), :], in_=Cblk[b].ap)],
                      r=[Cblk[b]], w=[("out", b, b + 1)])
          except _Stop:
            break
        final = [Tok(sem, p.dma_cnt[s]) for s, sem in p.dma_sems.items() if s.startswith("ost") or s == "dbg"]
        p.wait_only("sp", final)
        p.emit()
    return nc


def _host_consts(half, g_attn, g_ffn, g_ple, g_q, g_kv, conv_w, conv_b, g_final):
    pos = np.concatenate([np.arange(HALF), half * HALF + np.arange(HALF)]).astype(np.float64)
    if half == 0:
        pos[:HALF] = 0.0
    inv_r = 1.0 / (10000.0 ** (np.arange(0, 256, 2, dtype=np.float64) / 256))
    ang_r = inv_r[:, None] * pos[None, :]
    inv_m = 1.0 / (10000.0 ** (np.arange(0, 64, 2, dtype=np.float64) / 64))
    ang_m = inv_m[:, None] * pos[None, :]
    cosm = np.tile(np.cos(ang_m), (4, 1))
    sgn = np.where((np.arange(128) % 64) < 32, -1.0, 1.0)[:, None]
    sinm = np.tile(np.sin(ang_m), (4, 1)) * sgn
    c_cs = np.stack([np.cos(ang_r), np.sin(ang_r), cosm, sinm], axis=1).astype(np.float32)

    logg = np.log1p(-np.exp2(-5.0 - np.arange(8, dtype=np.float64)))
    i = np.arange(128)[:, None].astype(np.float64)
    n = np.arange(512)[None, :].astype(np.float64)
    ci, cn = (i // 64), (n // 64)
    dmqd = np.zeros((8, 128, 2, 512), np.float64)
    for h in range(8):
        same = np.exp(logg[h] * np.abs(n - i))
        later = np.exp(logg[h] * (n - i))
        dm = np.where(cn == ci, same, np.where(cn > ci, later, 0.0)) / 16.0
        dmqd[h, :, 0, :] = dm
        dmqd[h, :, 1, :] = np.exp(logg[h] * (n + 1.0))
    small = np.zeros((128, 1024), np.float64)
    fm = lambda v: np.asarray(v, np.float64).reshape(-1, 128).T
    small[:, 0:32] = fm(g_attn); small[:, 32:64] = fm(g_ffn); small[:, 64:96] = fm(g_ple)
    small[:, 96:104] = fm(g_q); small[:, 104:108] = fm(g_kv)
    small[:, 108:140] = 0.0
    if half == 0:
        small[:, 108:108 + 16] = NEG
    pidx = np.arange(128, dtype=np.float64)
    for li, L in enumerate((512, 384, 128)):
        for h in range(8):
            for b in range(4):
                small[:, 140 + li * 32 + h * 4 + b] = np.exp(logg[h] * (L - 1 - 128 * b - pidx)) / 16.0
    cw = np.asarray(conv_w, np.float64)
    for j in range(3):
        small[:, 236 + j:236 + 3 * NFF:3] = fm(cw[j])
    small[:, 494:494 + NFF] = fm(conv_b)
    small[:, 580] = EPS
    gfin = np.broadcast_to(np.asarray(g_final, np.float32)[None, :], (128, D)).copy()
    return c_cs, dmqd.astype(np.float32), small.astype(np.float32), gfin


_CACHE = {}


def _core_inputs(c, x, pfull, weights, consts):
    b, half = c // 2, c % 2
    xo = np.ascontiguousarray(x[b, half * HALF:(half + 1) * HALF])
    xp = np.ascontiguousarray(x[b, 0:HALF]) if half == 1 else np.zeros((HALF, D), np.float32)
    m = {"xo": xo, "xp": xp, "pp": np.ascontiguousarray(pfull[0, b, half * HALF:(half + 1) * HALF])}
    m.update(weights)
    c_cs, dmqd, small, gfin = consts[half]
    m.update({"c_cs": c_cs, "c_dmqd": dmqd, "c_small": small, "c_gfin": gfin, "c_ident": np.eye(128, dtype=np.float32)})
    return m


def kernel(x, p, w_in, g_attn, g_q_lora, g_kv_lora, w_uq, w_ukv, w_o, g_ffn, w_ffn_gate, w_ffn_up, conv_w, conv_b,
           w_ffn_down, g_ple, w_ple_gate, w_ple_proj, g_final):
    f = lambda a: np.ascontiguousarray(np.asarray(a, dtype=np.float32))
    x = f(x); p = f(p)
    weights = {"w_in": f(w_in)[0], "w_uq": f(w_uq)[0], "w_ukv": f(w_ukv)[0], "w_o": f(w_o)[0], "w_fg": f(w_ffn_gate)[0],
               "w_fu": f(w_ffn_up)[0], "w_fd": f(w_ffn_down)[0], "w_pg": f(w_ple_gate)[0], "w_pp": f(w_ple_proj)[0]}
    consts = [_host_consts(h, f(g_attn)[0], f(g_ffn)[0], f(g_ple)[0], f(g_q_lora)[0], f(g_kv_lora)[0], f(conv_w)[0], f(conv_b)[0],
                           f(g_final)) for h in range(2)]
    if "nc" not in _CACHE:
        _CACHE["nc"] = build_program()
    nc = _CACHE["nc"]
    in_maps = [_core_inputs(c, x, p, weights, consts) for c in range(8)]
    res = run_bass_kernel_spmd(nc, in_maps, core_ids=list(range(8)))
    outp = np.empty((4, SEQ, D), np.float32)
    for c in range(8):
        b, half = c // 2, c % 2
        outp[b, half * HALF:(half + 1) * HALF] = np.asarray(res.results[c]["out"], dtype=np.float32)
    return outp
```

```python
import numpy as np
from contextlib import ExitStack
import concourse.bass as bass
import concourse.mybir as mybir
from concourse.bass_utils import run_bass_kernel_spmd

F32 = mybir.dt.float32
BF16 = mybir.dt.bfloat16
AF = mybir.ActivationFunctionType
ALU = mybir.AluOpType

D = 4096
SEQ = 4096
HALF = 2048
INW = 9792
OFF_RQ, OFF_RK, OFF_RV, OFF_RG, OFF_CQ, OFF_CKV, OFF_KR = 0, 2048, 4096, 6144, 8192, 9216, 9728
QL, KVL, DFF, PLED = 1024, 512, 11008, 256
NFF = DFF // 128
EPS = 1e-6
SCALE = 192.0 ** -0.5
NEG = -30000.0
TILES = [(0, 512, "pre"), (512, 512, "pre"), (1024, 512, "pre"), (1536, 512, "halo"),
         (2048, 512, "own"), (2560, 512, "own"), (3072, 512, "own"), (3584, 512, "own")]
LIDX = {512: 0, 384: 1, 128: 2}
ENGS = ("pe", "act", "dve", "pool", "sp")


class Tok:
    __slots__ = ("sem", "val")

    def __init__(self, sem, val):
        self.sem = sem
        self.val = val


class Buf:
    def __init__(self, ap, arena, lo, hi):
        self.ap = ap
        self.reg = (arena, lo, hi)

    def __getitem__(self, k):
        return self.ap[k]

    def sub(self, lo, hi):
        return (self.reg[0], self.reg[1] + lo, self.reg[1] + hi)


def _regs(lst):
    out = []
    for x in lst:
        if x is None:
            continue
        if isinstance(x, Buf):
            out.append(x.reg)
        elif isinstance(x, list):
            out.extend(_regs(x))
        else:
            out.append(x)
    return out


class Prog:
    def __init__(self, nc, es):
        self.nc = nc
        self.es = es
        self.q = {e: [] for e in ENGS}
        self.psem = {e: es.enter_context(nc.semaphore("prog_" + e)) for e in ENGS}
        self.pcnt = {e: 0 for e in ENGS}
        self.seen = {e: {} for e in ENGS}
        self.dma_sems = {}
        self.dma_cnt = {}
        self.ar = {}
        self.nops = 0

    def _deps(self, reads, writes):
        toks = []
        for (a, lo, hi) in reads:
            d = self.ar.get(a)
            if d:
                for (l2, h2), ent in d.items():
                    if l2 < hi and lo < h2 and ent[0] is not None:
                        toks.append(ent[0])
        for (a, lo, hi) in writes:
            d = self.ar.get(a)
            if d:
                for (l2, h2), ent in d.items():
                    if l2 < hi and lo < h2:
                        if ent[0] is not None:
                            toks.append(ent[0])
                        toks.extend(ent[1].values())
        return toks

    def _commit(self, tok, reads, writes):
        for (a, lo, hi) in writes:
            d = self.ar.setdefault(a, {})
            for k in [k for k in d if lo <= k[0] and k[1] <= hi]:
                del d[k]
            d[(lo, hi)] = [tok, {}]
        for (a, lo, hi) in reads:
            d = self.ar.setdefault(a, {})
            ent = d.setdefault((lo, hi), [None, {}])
            ent[1][id(tok.sem)] = tok

    def _waits(self, eng, toks):
        out = []
        for t in toks:
            if t is None:
                continue
            if eng == "pe" and t.sem is self.psem["pe"]:
                continue
            k = id(t.sem)
            if self.seen[eng].get(k, -1) >= t.val:
                continue
            self.seen[eng][k] = t.val
            out.append((t.sem, t.val))
        return out

    def op(self, eng, fn, r=(), w=(), extra=()):
        reads, writes = _regs(r), _regs(w)
        wt = self._waits(eng, self._deps(reads, writes) + list(extra))
        self.pcnt[eng] += 1
        tok = Tok(self.psem[eng], self.pcnt[eng])
        self.q[eng].append((wt, [fn], (self.psem[eng], 1)))
        self._commit(tok, reads, writes)
        self.nops += 1
        return tok

    def group(self, eng, fns, r=(), w=(), extra=()):
        reads, writes = _regs(r), _regs(w)
        wt = self._waits(eng, self._deps(reads, writes) + list(extra))
        self.pcnt[eng] += 1
        tok = Tok(self.psem[eng], self.pcnt[eng])
        self.q[eng].append((wt, list(fns), (self.psem[eng], 1)))
        self._commit(tok, reads, writes)
        self.nops += len(fns)
        return tok

    def dma(self, eng, slot, fns, r=(), w=(), extra=()):
        if slot not in self.dma_sems:
            self.dma_sems[slot] = self.es.enter_context(self.nc.semaphore("dma_" + slot))
            self.dma_cnt[slot] = 0
        sem = self.dma_sems[slot]
        reads, writes = _regs(r), _regs(w)
        prev = [Tok(sem, self.dma_cnt[slot])] if self.dma_cnt[slot] else []
        wt = self._waits(eng, self._deps(reads, writes) + list(extra) + prev)
        tok = None
        for i, fn in enumerate(fns):
            self.dma_cnt[slot] += 16
            self.q[eng].append((wt if i == 0 else [], [fn], (sem, 16)))
        tok = Tok(sem, self.dma_cnt[slot])
        self._commit(tok, reads, writes)
        return tok

    def wait_only(self, eng, toks):
        wt = self._waits(eng, toks)
        if wt:
            self.q[eng].append((wt, [], None))

    def emit(self):
        nc = self.nc
        with nc.Block() as block:
            def run(name):
                def body(e):
                    for (wt, fns, inc) in self.q[name]:
                        for (sem, val) in wt:
                            e.wait_ge(sem, val)
                        ins = None
                        for fn in fns:
                            ins = fn(e)
                        if inc is not None and ins is not None:
                            ins.then_inc(inc[0], inc[1])
                return body
            block.tensor(run("pe"))
            block.scalar(run("act"))
            block.vector(run("dve"))
            block.gpsimd(run("pool"))
            block.sync(run("sp"))


class PsumPool:
    def __init__(self, nc, es):
        self.t = [es.enter_context(nc.psum_tensor(f"psb{i}", [128, 512], F32)) for i in range(8)]
        self.bufs = [Buf(self.t[i], "psum", i * 512, (i + 1) * 512) for i in range(8)]
        for i, b in enumerate(self.bufs):
            b.idx = i
            b.bf = self.t[i].bitcast(BF16)
        self.held = [False] * 8
        self.nxt = 0

    def alloc(self):
        for _ in range(8):
            i = self.nxt
            self.nxt = (self.nxt + 1) % 8
            if not self.held[i]:
                self.held[i] = True
                return self.bufs[i]
        raise RuntimeError("PSUM exhausted")

    def free(self, b):
        if isinstance(b, list):
            for x in b:
                self.free(x)
            return
        assert self.held[b.idx]
        self.held[b.idx] = False


STATS = {'mm': 0, 'mm_cols': 0, 'mm_cols_f32': 0, 'tr': 0, 'tr_f32': 0}


class _Stop(Exception):
    pass


def build_program(tiles=TILES, dbg=None, stop=None):
    nc = bass.Bass("TRN2", target_bir_lowering=False)

    def din(name, shape, dt=F32):
        return nc.dram_tensor(name, list(shape), dt, kind="ExternalInput").ap()

    xo = din("xo", [HALF, D]); xp = din("xp", [HALF, D]); pp = din("pp", [HALF, PLED])
    w_in = din("w_in", [D, INW]); w_uq = din("w_uq", [QL, 3072]); w_ukv = din("w_ukv", [KVL, 4096])
    w_o = din("w_o", [D, D]); w_fg = din("w_fg", [D, DFF]); w_fu = din("w_fu", [D, DFF])
    w_fd = din("w_fd", [DFF, D]); w_pg = din("w_pg", [D, D]); w_pp = din("w_pp", [PLED, D])
    c_cs = din("c_cs", [128, 4, SEQ])
    c_dmqd = din("c_dmqd", [8, 128, 2, 512])
    c_small = din("c_small", [128, 1024])
    c_gfin = din("c_gfin", [128, D])
    c_ident = din("c_ident", [128, 128])
    out = nc.dram_tensor("out", [HALF, D], F32, kind="ExternalOutput").ap()
    kc = nc.dram_tensor("kc_scratch", [16, 128, SEQ], BF16, kind="Internal").ap()
    vc = nc.dram_tensor("vc_scratch", [8, SEQ, 256], BF16, kind="Internal").ap()
    dbg_out = {}
    if dbg:
        for name, shape in dbg.items():
            dbg_out[name] = nc.dram_tensor("dbg_" + name, list(shape), F32, kind="ExternalOutput").ap()

    with ExitStack() as es:
        es.enter_context(nc.allow_low_precision("bf16 matmul operands, fp32 accumulation"))
        p = Prog(nc, es)
        ps = PsumPool(nc, es)

        def sb(name, shape, dt):
            return es.enter_context(nc.sbuf_tensor(name, list(shape), dt))

        A_t = sb("arenaA", [128, 16384], BF16)
        B_t = sb("arenaB", [128, 8192], F32)
        C_t = sb("arenaC", [128, 16384], F32)
        D_t = sb("arenaD", [128, 2048], F32)
        krT_t = sb("krT", [128, SEQ], BF16)
        S_t = sb("Sst", [128, 8 * 512], F32)
        small_t = sb("small", [128, 1024], F32)
        carry_t = sb("carry", [128, NFF * 2], F32)
        stat_t = sb("stat", [128, 64], F32)
        identF_t = sb("identF", [128, 128], F32)
        identB_t = sb("identB", [128, 128], BF16)
        onesF_t = sb("onesF", [128, 128], F32)
        onesB_t = sb("onesB", [128, 128], BF16)
        NW = 4
        wr_t = [sb(f"wring{i}", [128, 4096], BF16) for i in range(NW)]

        def vw(t, arena, lo, hi, dt=F32, pat=None, **kw):
            if arena == "A":
                ap = t[:, lo:hi]
            elif dt == F32:
                ap = t[:, lo:hi]
            else:
                ap = t.bitcast(BF16)[:, 2 * lo:2 * hi]
            if pat:
                ap = ap.rearrange(pat, **kw)
            return Buf(ap, arena, lo, hi)

        hnT = vw(A_t, "A", 0, 16384, BF16, "p (c t) -> p c t", c=32)
        concatT = vw(B_t, "B", 0, 8192, BF16, "p (c t) -> p c t", c=32)
        junk = vw(B_t, "B", 4096, 6144, BF16)
        Cblk = [vw(C_t, "C", b * 4096, (b + 1) * 4096) for b in range(4)]
        stgb = [vw(D_t, "D", i * 512, (i + 1) * 512, BF16) for i in range(4)]
        krT = Buf(krT_t, "krT", 0, SEQ)
        Sst = [Buf(S_t[:, h * 512:(h + 1) * 512], "S", h * 512, (h + 1) * 512) for h in range(8)]
        small = Buf(small_t, "small", 0, 1024)
        carry = Buf(carry_t[:, :].rearrange("p (c t) -> p c t", t=2), "carry", 0, NFF * 2)
        stat = Buf(stat_t, "stat", 0, 64)
        identF = Buf(identF_t, "identF", 0, 1); identB = Buf(identB_t, "identB", 0, 1)
        onesF = Buf(onesF_t, "onesF", 0, 1); onesB = Buf(onesB_t, "onesB", 0, 1)
        G_A, G_F, G_P, G_Q, G_KV, KB, KDEC, CW, CB, EPSC = 0, 32, 64, 96, 104, 108, 140, 236, 494, 580
        eps_ap = small_t[:, EPSC:EPSC + 1]

        qT = vw(C_t, "C", 0, 1024, BF16, "p (c t) -> p c t", c=4)
        qdT = vw(C_t, "C", 1024, 2048, BF16, "p (c t) -> p c t", c=4)
        kT = vw(C_t, "C", 2048, 3072, BF16, "p (c t) -> p c t", c=4)
        Vb = vw(C_t, "C", 3072, 4096, BF16, "p (b c) -> p b c", b=4)
        Vd = vw(C_t, "C", 4096, 5120, BF16, "p (b c) -> p b c", b=4)
        sgb = vw(C_t, "C", 5120, 7168, F32, "p (c t) -> p c t", c=4)
        T1 = vw(C_t, "C", 7168, 7680); T2 = vw(C_t, "C", 7680, 8192); R1 = vw(C_t, "C", 8192, 8704)
        ktok = vw(C_t, "C", 8704, 9216, BF16)
        scD = [vw(C_t, "C", 9216 + 256 * i, 9216 + 256 * (i + 1), BF16) for i in range(4)]
        o32 = vw(C_t, "C", 10240, 11264, F32, "p (c t) -> p c t", c=2)
        sq32 = vw(C_t, "C", 11264, 12288, F32, "p (c t) -> p c t", c=2)
        mu = vw(C_t, "C", 12288, 12800); rs = vw(C_t, "C", 12800, 13312); tm = vw(C_t, "C", 13312, 13824)
        Sbf = vw(C_t, "C", 13824, 14080, BF16)
        cs = vw(C_t, "C", 14336, 16384, F32, "p (k t) -> p k t", k=4)
        dmqd = vw(D_t, "D", 0, 2048, F32, "p (h k t) -> p h k t", h=2, k=2)
        cq32 = vw(C_t, "C", 0, 4096, F32, "p (c t) -> p c t", c=8)
        ckv32 = vw(C_t, "C", 4096, 6144, F32, "p (c t) -> p c t", c=4)
        sq16 = vw(C_t, "C", 6144, 8192, BF16, "p (c t) -> p c t", c=8)
        cqnT = vw(C_t, "C", 8192, 10240, BF16, "p (c t) -> p c t", c=8)
        ckvnT = vw(C_t, "C", 10240, 11264, BF16, "p (c t) -> p c t", c=4)
        rstdq = vw(C_t, "C", 11264, 11776); rstdkv = vw(C_t, "C", 11776, 12288)
        kra = vw(C_t, "C", 12288, 12800); krb = vw(C_t, "C", 12800, 13312)
        T1m = vw(C_t, "C", 13312, 13824); T2m = vw(C_t, "C", 13824, 14336)
        Kst = vw(C_t, "C", 0, 4096, BF16, "p (h t) -> p h t", h=16)
        Vst = vw(C_t, "C", 4096, 8192, BF16, "p (b c) -> p b c", b=4)
        kTp = [vw(C_t, "C", 0, 2048, BF16, "p (h t) -> p h t", h=2), vw(C_t, "C", 2048, 4096, BF16, "p (h t) -> p h t", h=2)]
        vvp = [vw(C_t, "C", 4096, 6144, BF16, "p (b c) -> p b c", c=256), vw(C_t, "C", 6144, 8192, BF16, "p (b c) -> p b c", c=256)]
        qn = vw(C_t, "C", 10240, 10752, BF16, "p (h t) -> p h t", h=2)
        qra = vw(C_t, "C", 10752, 11264); qrb = vw(C_t, "C", 11264, 11776)
        qr = vw(C_t, "C", 11776, 12032, BF16)
        PT = [vw(C_t, "C", 12032 + 256 * i, 12032 + 256 * (i + 1), BF16) for i in range(3)]
        rz = vw(C_t, "C", 12800, 13312)
        Wr = vw(D_t, "D", 0, 512, BF16, "p (k c) -> p k c", k=8)
        qrp1 = vw(D_t, "D", 512, 768, BF16)
        Qs = None
        PT = PT + [vw(D_t, "D", 1792, 2048, BF16), vw(C_t, "C", 14080, 14336, BF16)]
        qn2 = vw(D_t, "D", 768, 1280, BF16, "p (h t) -> p h t", h=2)
        qr2 = vw(D_t, "D", 1280, 1536, BF16)
        qrp1b = vw(D_t, "D", 1536, 1792, BF16)
        Qs = [(qn, qr, qrp1), (qn2, qr2, qrp1b)]
        actT = [vw(B_t, "B", 2048 * i, 2048 * (i + 1), BF16, "p (c t) -> p c t", c=8) for i in range(2)]
        NGX = 6
        gext = [vw(B_t, "B", 4096 + 514 * i, 4096 + 514 * (i + 1)) for i in range(NGX)]
        ft = [vw(D_t, "D", 512 * i, 512 * (i + 1)) for i in range(4)]
        gfin = vw(B_t, "B", 0, 4096)
        pbuf = vw(B_t, "B", 6144, 7168, F32, "p (b c) -> p b c", b=4)
        pT = vw(B_t, "B", 7168, 7680, BF16, "p (c t) -> p c t", c=2)
        ppo = [vw(D_t, "D", 512 * i, 512 * (i + 1)) for i in range(4)]
        sgm = [vw(B_t, "B", 7680 + 256 * 0, 7680 + 256 * 0 + 512)]
        sgm = [vw(C_t, "C", 0, 0)]
        sgt = [vw(B_t, "B", 4096, 4608), vw(B_t, "B", 4608, 5120)]

        def ACT(out_, in_, func, r, w, bias=None, scale=None, accum=None):
            kw = {}
            if bias is not None:
                kw["bias"] = bias
            if scale is not None:
                kw["scale"] = scale
            if accum is not None:
                kw["accum_out"] = accum
            return p.op("act", lambda e: e.activation(out=out_, in_=in_, func=func, **kw), r, w)

        def TT(out_, in0, in1, op, r, w, eng="dve"):
            return p.op(eng, lambda e: e.tensor_tensor(out=out_, in0=in0, in1=in1, op=op), r, w)

        def TS(out_, in0, s1, s2, op0, op1, r, w, eng="dve"):
            if op1 is None:
                return p.op(eng, lambda e: e.tensor_scalar(out=out_, in0=in0, scalar1=s1, scalar2=None, op0=op0), r, w)
            return p.op(eng, lambda e: e.tensor_scalar(out=out_, in0=in0, scalar1=s1, scalar2=s2, op0=op0, op1=op1), r, w)

        def STT(out_, in0, sc, in1, op0, op1, r, w, accum=None, eng="dve"):
            kw = {}
            if accum is not None:
                kw["accum_out"] = accum
            return p.op(eng, lambda e: e.scalar_tensor_tensor(out=out_, in0=in0, scalar=sc, in1=in1, op0=op0, op1=op1, **kw), r, w)

        def CP(out_, in_, r, w, eng="dve"):
            return p.op(eng, lambda e: e.tensor_copy(out=out_, in_=in_), r, w)

        def RCP(out_, in_, r, w):
            return p.op("dve", lambda e: e.reciprocal(out=out_, in_=in_), r, w)

        def MM(out_, lhsT, rhs, start, stop):
            STATS['mm'] += 1
            STATS['mm_cols'] += out_.shape[-1]
            if lhsT.dtype == F32:
                STATS['mm_cols_f32'] += out_.shape[-1]
            return lambda e: e.matmul(out_, lhsT, rhs, start=start, stop=stop)

        def TR(out_, in_, ident):
            STATS['tr'] += 1
            if in_.dtype == F32:
                STATS['tr_f32'] += 1
            return lambda e: e.transpose(out_, in_, ident)

        wstate = {"n": 0}

        def wload(parts, shape):
            i = wstate["n"] % NW
            wstate["n"] += 1
            t = wr_t[i]
            nk, ncol = shape
            view = t[:, 0:nk * ncol].rearrange("p (k c) -> p k c", k=nk)
            buf = Buf(view, "wring", i * 4096, (i + 1) * 4096)
            fns = []
            for (cs_, src) in parts:
                dst = view if cs_ is None else view[:, :, cs_[0]:cs_[1]]
                fns.append(lambda e, dst=dst, src=src: e.dma_start(out=dst, in_=src))
            p.dma("pool", f"w{i}", fns, r=[], w=[buf])
            return buf

        def wslab(w, k0, nk, c0, ncol):
            src = w.rearrange("(kc p) n -> p kc n", p=128)[:, k0:k0 + nk, c0:c0 + ncol]
            return wload([(None, src)], (nk, ncol))

        def fm_proj(w, c0, ncol, nkc, rhs_fn, rhs_regs, N, loader=None, extra=None):
            nch = ncol // 128
            banks = [ps.alloc() for _ in range(nch)]
            bx = ps.alloc() if extra else None
            for si, s0 in enumerate(range(0, nkc, 8)):
                sk = min(8, nkc - s0)
                slab = loader(s0, sk) if loader else wslab(w, s0, sk, c0, ncol)
                fns = []
                for j in range(nch):
                    for k in range(sk):
                        fns.append(MM(banks[j][:, 0:N], slab[:, k, j * 128:(j + 1) * 128], rhs_fn(s0 + k),
                                      s0 + k == 0, s0 + k == nkc - 1))
                if extra:
                    n2 = extra[2]
                    for j in range(nch):
                        col = (si * nch + j) * n2
                        for k in range(sk):
                            fns.append(MM(bx[:, col:col + n2], slab[:, k, j * 128:(j + 1) * 128], extra[0](s0 + k), k == 0, k == sk - 1))
                p.group("pe", fns, r=[slab] + rhs_regs + (extra[1] if extra else []), w=banks + ([bx] if extra else []))
            if extra:
                banks.append(bx)
            return banks

        def tm_proj(w, c0, ncol, k0, nkc, lhs_fn, lhs_regs, nb):
            banks = [ps.alloc() for _ in range(nb)]
            for s0 in range(0, nkc, 8):
                sk = min(8, nkc - s0)
                slab = wslab(w, k0 + s0, sk, c0, ncol)
                fns = []
                for b in range(nb):
                    for k in range(sk):
                        fns.append(MM(banks[b][:, 0:ncol], lhs_fn(s0 + k, b), slab[:, k, 0:ncol],
                                      s0 + k == 0, s0 + k == nkc - 1))
                p.group("pe", fns, r=[slab] + lhs_regs, w=banks)
            return banks

        p.dma("sp", "c0", [lambda e: e.dma_start(out=small_t[:, :], in_=c_small)], w=[small])
        p.dma("sp", "c1", [lambda e: e.dma_start(out=identF_t[:, :], in_=c_ident)], w=[identF])
        CP(identB_t[:, :], identF_t[:, :], [identF], [identB])
        p.op("dve", lambda e: e.memset(onesF_t[:, :], 1.0), [], [onesF])
        p.op("dve", lambda e: e.memset(onesB_t[:, :], 1.0), [], [onesB])
        p.op("dve", lambda e: e.memset(S_t[:, :], 0.0), [], Sst)
        p.op("dve", lambda e: e.memset(carry_t[:, :], 0.0), [], [carry])

        def norm_T(nb, gcol, blocks=None):
            it = 0
            for b in (blocks if blocks is not None else range(nb)):
                ACT(junk.ap, Cblk[b].ap, AF.Square, [Cblk[b]], [junk, stat.sub(b, b + 1)], accum=stat_t[:, b:b + 1])
                ACT(stat_t[:, 8 + b:9 + b], stat_t[:, b:b + 1], AF.Sqrt, [stat.sub(b, b + 1), small], [stat.sub(8 + b, 9 + b)],
                    bias=eps_ap, scale=1.0 / D)
                RCP(stat_t[:, 8 + b:9 + b], stat_t[:, 8 + b:9 + b], [stat.sub(8 + b, 9 + b)], [stat.sub(8 + b, 9 + b)])
                for q in range(4):
                    s = stgb[it % 4]
                    it += 1
                    ACT(s.ap, Cblk[b][:, q * 1024:(q + 1) * 1024], AF.Copy, [Cblk[b], stat.sub(8 + b, 9 + b)], [s],
                        scale=stat_t[:, 8 + b:9 + b])
                    bk = ps.alloc()
                    fns = [TR(bk.bf[:, i * 128:(i + 1) * 128], s[:, i * 128:(i + 1) * 128], identB_t[:, :]) for i in range(8)]
                    p.group("pe", fns, r=[s, identB], w=[bk])
                    c0 = q * 8
                    gb = small_t[:, gcol + c0:gcol + c0 + 8].unsqueeze(2).broadcast_to([128, 8, 128])
                    TT(hnT[:, c0:c0 + 8, b * 128:(b + 1) * 128], bk.bf[:, 0:1024].rearrange("p (c t) -> p c t", c=8), gb,
                       ALU.mult, [bk, small], [hnT])
                    ps.free(bk)

        def load_x(ltok0, L):
            nb = L // 128
            for b in range(nb):
                t0_ = ltok0 + b * 128
                src = (xp[t0_:t0_ + 128, :] if t0_ < HALF else xo[t0_ - HALF:t0_ - HALF + 128, :])
                p.dma("sp", f"xld{b}", [lambda e, src=src, b=b: e.dma_start(out=Cblk[b].ap, in_=src)], r=[], w=[Cblk[b]])

        def rope_pair(banks, N, outT, qd, toff):
            cosr, sinr = cs[:, 0, toff:toff + N], cs[:, 1, toff:toff + N]
            for hi in range(2):
                x1, x2 = banks[2 * hi], banks[2 * hi + 1]
                for half in range(2):
                    a, b_ = (x1, x2) if half == 0 else (x2, x1)
                    TT(T1[:, 0:N], a[:, 0:N], cosr, ALU.mult, [a, cs], [T1])
                    TT(T2[:, 0:N], b_[:, 0:N], sinr, ALU.mult, [b_, cs], [T2])
                    op = ALU.subtract if half == 0 else ALU.add
                    if qd is None:
                        TT(outT[:, 2 * hi + half, 0:N], T1[:, 0:N], T2[:, 0:N], op, [T1, T2], [outT])
                    else:
                        TT(R1[:, 0:N], T1[:, 0:N], T2[:, 0:N], op, [T1, T2], [R1])
                        ACT(outT[:, 2 * hi + half, 0:N], R1[:, 0:N], AF.Copy, [R1], [outT])
                        TT(qd[:, 2 * hi + half, 0:N], R1[:, 0:N], dmqd[:, hi, 1, 0:N], ALU.mult, [R1, dmqd], [qd])

        def hn_rhs(N0, N1):
            return lambda k: hnT[:, k, N0:N1]

        hhalo_t = sb("hhalo", [128, 64], BF16)
        hhalo = Buf(hhalo_t[:, :].rearrange("p (c t) -> p c t", t=2), "hhalo", 0, 64)

        def chk(k):
            if stop == k:
                raise _Stop()

        first_own = True
        for (ltok0, L, mode) in tiles:
          try:
            nb = L // 128
            hasq = mode != "pre"
            q0 = 384 if mode == "halo" else 0
            Lq = (L - q0) if hasq else 0
            qb0 = q0 // 128
            nbq = Lq // 128
            load_x(ltok0, L)
            norm_T(nb, G_A)
            p.dma("sp", "csld", [lambda e, l0=ltok0, L=L: e.dma_start(out=cs[:, :, 0:L], in_=c_cs[:, :, l0:l0 + L])], r=[], w=[cs])

            chk(1)
            for hp in range(4):
                h0 = 2 * hp
                if hasq:
                    p.dma("sp", "dmqd", [lambda e, h0=h0: e.dma_start(out=dmqd.ap, in_=c_dmqd[h0:h0 + 2].rearrange("h p k t -> p h k t"))],
                          r=[], w=[dmqd])
                bk_ = fm_proj(w_in, OFF_RK + 512 * hp, 512, 32, hn_rhs(0, L), [hnT], L)
                rope_pair(bk_, L, kT, None, 0)
                ps.free(bk_)
                if hasq:
                    bq_ = fm_proj(w_in, OFF_RQ + 512 * hp, 512, 32, hn_rhs(q0, L), [hnT], Lq)
                    rope_pair(bq_, Lq, qT, qdT, q0)
                    ps.free(bq_)
                bv_ = tm_proj(w_in, OFF_RV + 512 * hp, 512, 0, 32, lambda k, b: hnT[:, k, b * 128:(b + 1) * 128], [hnT], nb)
                for b in range(nb):
                    if hasq and b >= qb0:
                        ACT(Vb[:, b - qb0, :], bv_[b][:, :], AF.Copy, [bv_[b]], [Vb])
                    if b < qb0:
                        lidx, bb = LIDX[q0], b
                    else:
                        lidx, bb = LIDX[L - q0], b - qb0
                    for hi in range(2):
                        col = KDEC + lidx * 32 + (h0 + hi) * 4 + bb
                        ACT(Vd[:, b, hi * 256:(hi + 1) * 256], bv_[b][:, hi * 256:(hi + 1) * 256], AF.Copy, [bv_[b], small], [Vd],
                            scale=small_t[:, col:col + 1])
                ps.free(bv_)
                if hasq:
                    bg_ = fm_proj(w_in, OFF_RG + 512 * hp, 512, 32, hn_rhs(q0, L), [hnT], Lq)
                    for j in range(4):
                        ACT(sgb[:, j, 0:Lq], bg_[j][:, 0:Lq], AF.Silu, [bg_[j]], [sgb])
                    ps.free(bg_)
                for hi in range(2):
                    h = h0 + hi
                    bk = ps.alloc()
                    fns = [TR(bk.bf[:, b * 256 + dc * 128:b * 256 + (dc + 1) * 128], kT[:, 2 * hi + dc, b * 128:(b + 1) * 128], identB_t[:, :])
                           for b in range(nb) for dc in range(2)]
                    p.group("pe", fns, r=[kT, identB], w=[bk])
                    ACT(ktok[:, 0:nb * 256], bk.bf[:, 0:nb * 256], AF.Copy, [bk], [ktok])
                    ps.free(bk)

                    def state_update(b_lo, b_hi, ntok, h=h, hi=hi):
                        bu = ps.alloc()
                        fns = []
                        for dc in range(2):
                            for b in range(b_lo, b_hi):
                                fns.append(MM(bu[:, dc * 256:(dc + 1) * 256], ktok[:, b * 256 + dc * 128:b * 256 + (dc + 1) * 128],
                                              Vd[:, b, hi * 256:(hi + 1) * 256], b == b_lo, b == b_hi - 1))
                        p.group("pe", fns, r=[ktok, Vd], w=[bu])
                        sdec = float(np.exp(np.log1p(-2.0 ** (-5.0 - h)) * ntok))
                        STT(Sst[h].ap, Sst[h].ap, sdec, bu[:, :], ALU.mult, ALU.add, [Sst[h], bu], [Sst[h]])
                        ps.free(bu)

                    if qb0 > 0:
                        state_update(0, qb0, q0)
                    if hasq:
                        ACT(Sbf.ap, Sst[h].ap, AF.Copy, [Sst[h]], [Sbf])
                        for kb in range(nbq):
                            N = Lq - 128 * kb
                            bs = ps.alloc()
                            fns = [MM(bs[:, 0:N], kT[:, 2 * hi + dc, q0 + kb * 128:q0 + (kb + 1) * 128], qT[:, 2 * hi + dc, 128 * kb:Lq], dc == 0, dc == 1)
                                   for dc in range(2)]
                            p.group("pe", fns, r=[kT, qT], w=[bs])
                            TT(scD[kb][:, 0:N], bs[:, 0:N], dmqd[:, hi, 0, 0:N], ALU.mult, [bs, dmqd], [scD[kb]])
                            ps.free(bs)
                        for vcx in range(2):
                            bo = ps.alloc()
                            fns = [MM(bo[:, 0:Lq], Sbf[:, dc * 256 + vcx * 128:dc * 256 + (vcx + 1) * 128], qdT[:, 2 * hi + dc, 0:Lq], dc == 0, False)
                                   for dc in range(2)]
                            for kb in range(nbq):
                                N = Lq - 128 * kb
                                fns.append(MM(bo[:, 128 * kb:Lq], Vb[:, kb, hi * 256 + vcx * 128:hi * 256 + (vcx + 1) * 128], scD[kb][:, 0:N],
                                              False, kb == nbq - 1))
                            p.group("pe", fns, r=[Sbf, qdT, Vb] + scD[0:nbq], w=[bo])
                            ACT(o32[:, vcx, 0:Lq], bo[:, 0:Lq], AF.Copy, [bo], [o32])
                            ACT(sq32[:, vcx, 0:Lq], bo[:, 0:Lq], AF.Square, [bo], [sq32])
                            ps.free(bo)
                        bm = ps.alloc()
                        p.group("pe", [MM(bm[:, 0:Lq], onesF_t[:, :], o32[:, v, 0:Lq], v == 0, v == 1) for v in range(2)], r=[onesF, o32], w=[bm])
                        bq2 = ps.alloc()
                        p.group("pe", [MM(bq2[:, 0:Lq], onesF_t[:, :], sq32[:, v, 0:Lq], v == 0, v == 1) for v in range(2)], r=[onesF, sq32], w=[bq2])
                        ACT(mu[:, 0:Lq], bm[:, 0:Lq], AF.Copy, [bm], [mu], scale=1.0 / 256)
                        ps.free(bm)
                        TT(tm[:, 0:Lq], mu[:, 0:Lq], mu[:, 0:Lq], ALU.mult, [mu], [tm])
                        STT(rs[:, 0:Lq], bq2[:, 0:Lq], 1.0 / 256, tm[:, 0:Lq], ALU.mult, ALU.subtract, [bq2, tm], [rs])
                        ps.free(bq2)
                        ACT(rs[:, 0:Lq], rs[:, 0:Lq], AF.Ln, [rs, small], [rs], bias=eps_ap, scale=1.0)
                        ACT(rs[:, 0:Lq], rs[:, 0:Lq], AF.Exp, [rs], [rs], scale=-0.5)
                        for vcx in range(2):
                            TT(tm[:, 0:Lq], o32[:, vcx, 0:Lq], mu[:, 0:Lq], ALU.subtract, [o32, mu], [tm])
                            TT(tm[:, 0:Lq], tm[:, 0:Lq], rs[:, 0:Lq], ALU.mult, [tm, rs], [tm])
                            TT(concatT[:, 2 * h + vcx, 0:Lq], tm[:, 0:Lq], sgb[:, 2 * hi + vcx, 0:Lq], ALU.mult, [tm, sgb], [concatT])
                    state_update(qb0, nb, L - q0)

            chk(2)
            bc_ = fm_proj(w_in, OFF_CKV, 512, 32, hn_rhs(0, L), [hnT], L)
            for j in range(4):
                ACT(ckv32[:, j, 0:L], bc_[j][:, 0:L], AF.Copy, [bc_[j]], [ckv32])
                ACT(sq16[:, j, 0:L], bc_[j][:, 0:L], AF.Square, [bc_[j]], [sq16])
            ps.free(bc_)
            bst = ps.alloc()
            p.group("pe", [MM(bst[:, 0:L], onesB_t[:, :], sq16[:, j, 0:L], j == 0, j == 3) for j in range(4)], r=[onesB, sq16], w=[bst])
            ACT(rstdkv[:, 0:L], bst[:, 0:L], AF.Ln, [bst, small], [rstdkv], bias=eps_ap, scale=1.0 / KVL)
            ps.free(bst)
            ACT(rstdkv[:, 0:L], rstdkv[:, 0:L], AF.Exp, [rstdkv], [rstdkv], scale=-0.5)
            for j in range(4):
                STT(ckvnT[:, j, 0:L], ckv32[:, j, 0:L], small_t[:, G_KV + j:G_KV + j + 1], rstdkv[:, 0:L], ALU.mult, ALU.mult,
                    [ckv32, small, rstdkv], [ckvnT])
            w_kr = w_in.rearrange("(kc p) n -> p kc n", p=128)

            def kr_loader(s0, sk):
                src = w_kr[:, s0:s0 + sk, OFF_KR:OFF_KR + 64]
                return wload([((0, 64), src), ((64, 128), src)], (sk, 128))

            bkr = fm_proj(None, 0, 128, 32, hn_rhs(0, L), [hnT], L, loader=kr_loader)
            ACT(kra[:, 0:L], bkr[0][:, 0:L], AF.Copy, [bkr[0]], [kra])
            ps.free(bkr)
            for (d0, s0_) in ((0, 32), (32, 0), (64, 96), (96, 64)):
                CP(krb.ap[d0:d0 + 32, 0:L], kra.ap[s0_:s0_ + 32, 0:L], [kra], [krb])
            TT(T1m[:, 0:L], kra[:, 0:L], cs[:, 2, 0:L], ALU.mult, [kra, cs], [T1m])
            TT(T2m[:, 0:L], krb[:, 0:L], cs[:, 3, 0:L], ALU.mult, [krb, cs], [T2m])
            TT(krT_t[:, ltok0:ltok0 + L], T1m[:, 0:L], T2m[:, 0:L], ALU.add, [T1m, T2m], [krT.sub(ltok0, ltok0 + L)])
            if hasq:
                for g in range(2):
                    bcq = fm_proj(w_in, OFF_CQ + 512 * g, 512, 32, hn_rhs(q0, L), [hnT], Lq)
                    for j in range(4):
                        ACT(cq32[:, 4 * g + j, 0:Lq], bcq[j][:, 0:Lq], AF.Copy, [bcq[j]], [cq32])
                        ACT(sq16[:, 4 * g + j, 0:Lq], bcq[j][:, 0:Lq], AF.Square, [bcq[j]], [sq16])
                    ps.free(bcq)
                bst = ps.alloc()
                p.group("pe", [MM(bst[:, 0:Lq], onesB_t[:, :], sq16[:, j, 0:Lq], j == 0, j == 7) for j in range(8)], r=[onesB, sq16], w=[bst])
                ACT(rstdq[:, 0:Lq], bst[:, 0:Lq], AF.Ln, [bst, small], [rstdq], bias=eps_ap, scale=1.0 / QL)
                ps.free(bst)
                ACT(rstdq[:, 0:Lq], rstdq[:, 0:Lq], AF.Exp, [rstdq], [rstdq], scale=-0.5)
                for j in range(8):
                    STT(cqnT[:, j, 0:Lq], cq32[:, j, 0:Lq], small_t[:, G_Q + j:G_Q + j + 1], rstdq[:, 0:Lq], ALU.mult, ALU.mult,
                        [cq32, small, rstdq], [cqnT])

            chk(3)
            for s in range(4):
                slab = wslab(w_ukv, 0, 4, 1024 * s, 1024)
                for hh in range(4):
                    bk = ps.alloc()
                    p.group("pe", [MM(bk[:, 0:L], slab[:, k, hh * 256:hh * 256 + 128], ckvnT[:, k, 0:L], k == 0, k == 3) for k in range(4)],
                            r=[slab, ckvnT], w=[bk])
                    ACT(Kst[:, 4 * s + hh, 0:L], bk[:, 0:L], AF.Copy, [bk], [Kst])
                    ps.free(bk)
                for b in range(nb):
                    bv = ps.alloc()
                    fns = []
                    for hh in range(4):
                        for k in range(4):
                            fns.append(MM(bv[:, hh * 128:(hh + 1) * 128], ckvnT[:, k, b * 128:(b + 1) * 128],
                                          slab[:, k, hh * 256 + 128:hh * 256 + 256], k == 0, k == 3))
                    p.group("pe", fns, r=[slab, ckvnT], w=[bv])
                    CP(Vst[:, b, 512 * s:512 * (s + 1)], bv[:, :], [bv], [Vst])
                    ps.free(bv)
            kreg, vreg = ("kc", 0, 1), ("vc", 0, 1)
            p.dma("sp", "kvst", [lambda e, l0=ltok0, L=L: e.dma_start(out=kc[:, :, l0:l0 + L].rearrange("h d t -> d h t"), in_=Kst[:, :, 0:L])],
                  r=[Kst], w=[kreg])
            fns = []
            for b in range(nb):
                fns.append(lambda e, b=b, l0=ltok0: e.dma_start(
                    out=vc[:, l0 + b * 128:l0 + (b + 1) * 128, :].rearrange("q p c -> p q c"),
                    in_=Vst[:, b, :].rearrange("p (q c) -> p q c", q=8)))
            p.dma("sp", "kvst2", fns, r=[Vst], w=[vreg])

            chk(4)
            if not hasq:
                continue

            nkb = (ltok0 + L) // 128
            kh = nkb // 2
            dblk0 = (ltok0 + q0) // 128
            pti = 0

            def qproj(hp_):
                qn_b, qr_b, qrp_b = Qs[hp_ % 2]
                slab = wslab(w_uq, 0, 8, 384 * hp_, 384)
                CP(Wr[:, :, 0:64], slab[:, :, 128:192], [slab], [Wr])
                CP(Wr[:, :, 64:128], slab[:, :, 320:384], [slab], [Wr])
                for hi in range(2):
                    bqn = ps.alloc()
                    p.group("pe", [MM(bqn[:, 0:Lq], slab[:, k, hi * 192:hi * 192 + 128], cqnT[:, k, 0:Lq], k == 0, k == 7) for k in range(8)],
                            r=[slab, cqnT], w=[bqn])
                    ACT(qn_b[:, hi, 0:Lq], bqn[:, 0:Lq], AF.Copy, [bqn], [qn_b])
                    ps.free(bqn)
                bqr = ps.alloc()
                p.group("pe", [MM(bqr[:, 0:Lq], Wr[:, k, :], cqnT[:, k, 0:Lq], k == 0, k == 7) for k in range(8)], r=[Wr, cqnT], w=[bqr])
                ACT(qra[:, 0:Lq], bqr[:, 0:Lq], AF.Copy, [bqr], [qra])
                ps.free(bqr)
                for (d0, s0_) in ((0, 32), (32, 0), (64, 96), (96, 64)):
                    CP(qrb.ap[d0:d0 + 32, 0:Lq], qra.ap[s0_:s0_ + 32, 0:Lq], [qra], [qrb])
                TT(T1m[:, 0:Lq], qra[:, 0:Lq], cs[:, 2, q0:q0 + Lq], ALU.mult, [qra, cs], [T1m])
                TT(T2m[:, 0:Lq], qrb[:, 0:Lq], cs[:, 3, q0:q0 + Lq], ALU.mult, [qrb, cs], [T2m])
                TT(qr_b[:, 0:Lq], T1m[:, 0:Lq], T2m[:, 0:Lq], ALU.add, [T1m, T2m], [qr_b])
                p.op("dve", lambda e: e.memset(qrp_b.ap[0:64, 0:Lq], 0.0), [], [qrp_b])
                CP(qrp_b.ap[64:128, 0:Lq], qr_b.ap[64:128, 0:Lq], [qr_b], [qrp_b])
                p.op("dve", lambda e: e.memset(qr_b.ap[64:128, 0:Lq], 0.0), [qrp_b], [qr_b])

            qproj(0)
            for hp in range(8):
                for half, (b0, b1) in enumerate(((0, kh), (kh, nkb))):
                    nk_ = (b1 - b0) * 128
                    p.dma("sp", f"kld{half}", [lambda e, hp=hp, b0=b0, nk_=nk_, half=half: e.dma_start(
                        out=kTp[half][:, :, 0:nk_], in_=kc[2 * hp:2 * hp + 2, :, b0 * 128:b0 * 128 + nk_].rearrange("h d t -> d h t"))],
                        r=[kreg], w=[kTp[half]])
                    p.dma("sp", f"vld{half}", [lambda e, hp=hp, b0=b0, b1=b1, half=half: e.dma_start(
                        out=vvp[half][:, 0:b1 - b0, :], in_=vc[hp, b0 * 128:b1 * 128, :].rearrange("(b p) c -> p b c", p=128))],
                        r=[vreg], w=[vvp[half]])
                qn_c, qr_c, qrp_c = Qs[hp % 2]
                if hp < 7:
                    qproj(hp + 1)
                bo2 = [ps.alloc(), ps.alloc()]
                bz2 = [ps.alloc(), ps.alloc()]

                def scores(kb, hi):
                    nonlocal pti
                    half = 0 if kb < kh else 1
                    kl = kb - (0 if half == 0 else kh)
                    diag = kb >= dblk0
                    qlo = 128 * (kb - dblk0) if diag else 0
                    N = Lq - qlo
                    bs = ps.alloc()
                    fns = [MM(bs[:, 0:N], kTp[half][:, hi, kl * 128:(kl + 1) * 128], qn_c[:, hi, qlo:Lq], True, False),
                           MM(bs[:, 0:N], krT_t[:, kb * 128:(kb + 1) * 128], (qr_c if hi == 0 else qrp_c)[:, qlo:Lq], False, True)]
                    p.group("pe", fns, r=[kTp[half], qn_c, krT.sub(kb * 128, (kb + 1) * 128), qr_c, qrp_c], w=[bs])
                    pt = PT[pti % len(PT)]
                    pti += 1
                    if diag:
                        ACT(pt[:, 0:N], bs[:, 0:N], AF.Exp, [bs], [pt], scale=SCALE)
                        p.op("dve", lambda e, pt=pt: e.memset(pt.ap[64:128, 0:64], 0.0), [], [pt])
                    else:
                        ACT(pt[:, 0:N], bs[:, 0:N], AF.Exp, [bs, small], [pt], scale=SCALE, bias=small_t[:, KB + kb:KB + kb + 1])
                    ps.free(bs)
                    return (pt, N, qlo, half, kl)

                nxt = [scores(0, 0), scores(0, 1)]
                for kb in range(nkb):
                    cur = nxt
                    if kb + 1 < nkb:
                        nxt = [scores(kb + 1, 0), scores(kb + 1, 1)]
                    for hi in range(2):
                        pt, N, qlo, half, kl = cur[hi]
                        p.group("pe", [MM(bo2[hi][:, qlo:Lq], vvp[half][:, kl, hi * 128:(hi + 1) * 128], pt[:, 0:N], kb == 0, kb == nkb - 1),
                                       MM(bz2[hi][:, qlo:Lq], onesB_t[:, :], pt[:, 0:N], kb == 0, kb == nkb - 1)],
                                r=[vvp[half], pt, onesB], w=[bo2[hi], bz2[hi]])
                rzb = [rz, T2m]
                for hi in range(2):
                    ACT(rzb[hi][:, 0:Lq], bz2[hi][:, 0:Lq], AF.Ln, [bz2[hi]], [rzb[hi]])
                    ps.free(bz2[hi])
                for hi in range(2):
                    ACT(rzb[hi][:, 0:Lq], rzb[hi][:, 0:Lq], AF.Exp, [rzb[hi]], [rzb[hi]], scale=-1.0)
                for hi in range(2):
                    h = 2 * hp + hi
                    TT(concatT[:, 16 + h, 0:Lq], bo2[hi][:, 0:Lq], rzb[hi][:, 0:Lq], ALU.mult, [bo2[hi], rzb[hi]], [concatT])
                    ps.free(bo2[hi])

            chk(5)
            load_x(ltok0 + q0, Lq)
            for n in range(8):
                bw = tm_proj(w_o, 512 * n, 512, 0, 32, lambda k, b: concatT[:, k, b * 128:(b + 1) * 128], [concatT], nbq)
                for b in range(nbq):
                    reg = Cblk[b].sub(512 * n, 512 * (n + 1))
                    TT(Cblk[b][:, 512 * n:512 * (n + 1)], bw[b][:, :], Cblk[b][:, 512 * n:512 * (n + 1)], ALU.add, [bw[b], reg], [reg])
                ps.free(bw)
            if dbg and "h1" in dbg and mode == "own" and ltok0 == HALF:
                p.dma("sp", "dbg", [lambda e: e.dma_start(out=dbg_out["h1"], in_=Cblk[0].ap)], r=[Cblk[0]], w=[])

            chk(6)
            norm_T(nbq, G_F)
            if mode == "halo":
                CP(hhalo.ap, hnT[:, :, Lq - 2:Lq], [hnT], [hhalo])
                continue
            use_halo = first_own
            first_own = False
            gxs = {"i": 0}

            def GU(sgi):
                sg0 = sgi * 8
                nsg = min(8, NFF - sg0)
                aT = actT[sgi % 2]
                for g0 in range(sg0, sg0 + nsg, 4):
                    ng = min(4, sg0 + nsg - g0)
                    extra = (lambda k: hhalo[:, k, :], [hhalo], 2) if use_halo else None
                    gb_ = fm_proj(w_fg, g0 * 128, ng * 128, 32, hn_rhs(0, L), [hnT], L, extra=extra)
                    bx = gb_.pop() if use_halo else None
                    ges = []
                    for j in range(ng):
                        c = g0 + j
                        ge = gext[gxs["i"] % NGX]
                        gxs["i"] += 1
                        ges.append(ge)
                        if use_halo:
                            tb = stat.sub(16 + 8 * (j % 2), 24 + 8 * (j % 2))
                            t0c = 16 + 8 * (j % 2)
                            for si in range(4):
                                col = (si * ng + j) * 2
                                ACT(stat_t[:, t0c + 2 * si:t0c + 2 * si + 2], bx[:, col:col + 2], AF.Copy, [bx], [tb])
                            TT(ge[:, 0:2], stat_t[:, t0c:t0c + 2], stat_t[:, t0c + 2:t0c + 4], ALU.add, [tb], [ge])
                            TT(ge[:, 0:2], ge[:, 0:2], stat_t[:, t0c + 4:t0c + 6], ALU.add, [tb, ge], [ge])
                            TT(ge[:, 0:2], ge[:, 0:2], stat_t[:, t0c + 6:t0c + 8], ALU.add, [tb, ge], [ge])
                        else:
                            ACT(ge[:, 0:2], carry[:, c, :], AF.Copy, [carry], [ge])
                        ACT(ge[:, 2:2 + L], gb_[j][:, 0:L], AF.Copy, [gb_[j]], [ge])
                        ACT(carry[:, c, :], ge[:, L:L + 2], AF.Copy, [ge], [carry])
                    ps.free(gb_)
                    if bx is not None:
                        ps.free(bx)
                    for j in range(ng):
                        c = g0 + j
                        ge = ges[j]
                        t1 = ft[j]
                        cw = lambda i, c=c: small_t[:, CW + 3 * c + i:CW + 3 * c + i + 1]
                        TS(t1[:, 0:L], ge[:, 0:L], cw(0), small_t[:, CB + c:CB + c + 1], ALU.mult, ALU.add, [ge, small], [t1])
                        STT(t1[:, 0:L], ge[:, 1:L + 1], cw(1), t1[:, 0:L], ALU.mult, ALU.add, [ge, small, t1], [t1])
                        STT(t1[:, 0:L], ge[:, 2:L + 2], cw(2), t1[:, 0:L], ALU.mult, ALU.add, [ge, small, t1], [t1])
                        ACT(t1[:, 0:L], t1[:, 0:L], AF.Silu, [t1], [t1])
                    ub_ = fm_proj(w_fu, g0 * 128, ng * 128, 32, hn_rhs(0, L), [hnT], L)
                    for j in range(ng):
                        c = g0 + j
                        TT(aT[:, c - sg0, 0:L], ft[j][:, 0:L], ub_[j][:, 0:L], ALU.mult, [ft[j], ub_[j]], [aT])
                    ps.free(ub_)

            def DN(sgi):
                sg0 = sgi * 8
                nsg = min(8, NFF - sg0)
                aT = actT[sgi % 2]
                for n in range(8):
                    bd = tm_proj(w_fd, 512 * n, 512, sg0, nsg, lambda k, b, aT=aT: aT[:, k, b * 128:(b + 1) * 128], [aT], nb)
                    for b in range(nb):
                        reg = Cblk[b].sub(512 * n, 512 * (n + 1))
                        TT(Cblk[b][:, 512 * n:512 * (n + 1)], bd[b][:, :], Cblk[b][:, 512 * n:512 * (n + 1)], ALU.add, [bd[b], reg], [reg])
                    ps.free(bd)

            nsgs = (NFF + 7) // 8
            GU(0)
            for sgi in range(1, nsgs):
                GU(sgi)
                DN(sgi - 1)
            DN(nsgs - 1)
            if dbg and "h2" in dbg and ltok0 == HALF:
                p.dma("sp", "dbg", [lambda e: e.dma_start(out=dbg_out["h2"], in_=Cblk[0].ap)], r=[Cblk[0]], w=[])

            chk(7)
            t0 = ltok0 - HALF
            p.dma("sp", "pld", [lambda e, t0=t0: e.dma_start(out=pbuf.ap, in_=pp[t0:t0 + L, :].rearrange("(b p) c -> p b c", p=128))],
                  r=[], w=[pbuf])
            for kc_ in range(2):
                bk = ps.alloc()
                p.group("pe", [TR(bk[:, b * 128:(b + 1) * 128], pbuf[:, b, kc_ * 128:(kc_ + 1) * 128], identF_t[:, :]) for b in range(nb)],
                        r=[pbuf, identF], w=[bk])
                ACT(pT[:, kc_, 0:L], bk[:, 0:L], AF.Copy, [bk], [pT])
                ps.free(bk)
            norm_T(nb, G_P)
            p.dma("sp", "gfin", [lambda e: e.dma_start(out=gfin.ap, in_=c_gfin)], r=[], w=[gfin])
            for n in range(8):
                bp = tm_proj(w_pp, 512 * n, 512, 0, 2, lambda k, b: pT[:, k, b * 128:(b + 1) * 128], [pT], nb)
                for b in range(nb):
                    ACT(ppo[b].ap, bp[b][:, :], AF.Copy, [bp[b]], [ppo[b]])
                ps.free(bp)
                bg = tm_proj(w_pg, 512 * n, 512, 0, 32, lambda k, b: hnT[:, k, b * 128:(b + 1) * 128], [hnT], nb)
                for b in range(nb):
                    s_ = sgt[b % 2]
                    ACT(s_.ap, bg[b][:, :], AF.Sigmoid, [bg[b]], [s_])
                    TT(s_.ap, s_.ap, ppo[b].ap, ALU.mult, [s_, ppo[b]], [s_])
                    reg = Cblk[b].sub(512 * n, 512 * (n + 1))
                    TT(Cblk[b][:, 512 * n:512 * (n + 1)], s_.ap, Cblk[b][:, 512 * n:512 * (n + 1)], ALU.add, [s_, reg], [reg])
                ps.free(bg)

            chk(8)
            for b in range(nb):
                ACT(junk.ap, Cblk[b].ap, AF.Square, [Cblk[b]], [junk, stat.sub(b, b + 1)], accum=stat_t[:, b:b + 1])
                ACT(stat_t[:, 8 + b:9 + b], stat_t[:, b:b + 1], AF.Sqrt, [stat.sub(b, b + 1), small], [stat.sub(8 + b, 9 + b)],
                    bias=eps_ap, scale=1.0 / D)
                RCP(stat_t[:, 8 + b:9 + b], stat_t[:, 8 + b:9 + b], [stat.sub(8 + b, 9 + b)], [stat.sub(8 + b, 9 + b)])
                STT(Cblk[b].ap, Cblk[b].ap, stat_t[:, 8 + b:9 + b], gfin.ap, ALU.mult, ALU.mult, [Cblk[b], stat.sub(8 + b, 9 + b), gfin], [Cblk[b]])
                p.dma("sp", f"ost{b}", [lambda e, t0=t0, b=b: e.dma_start(out=out[t0 + 128 * b:t0 + 128 * (b + 1), :], in_=Cblk[b].ap)],
                      r=[Cblk[b]], w=[("out", b, b + 1)])
          except _Stop:
            break
        final = [Tok(sem, p.dma_cnt[s]) for s, sem in p.dma_sems.items() if s.startswith("ost") or s == "dbg"]
        p.wait_only("sp", final)
        p.emit()
    return nc


def _host_consts(half, g_attn, g_ffn, g_ple, g_q, g_kv, conv_w, conv_b, g_final):
    pos = np.concatenate([np.arange(HALF), half * HALF + np.arange(HALF)]).astype(np.float64)
    if half == 0:
        pos[:HALF] = 0.0
    inv_r = 1.0 / (10000.0 ** (np.arange(0, 256, 2, dtype=np.float64) / 256))
    ang_r = inv_r[:, None] * pos[None, :]
    inv_m = 1.0 / (10000.0 ** (np.arange(0, 64, 2, dtype=np.float64) / 64))
    ang_m = inv_m[:, None] * pos[None, :]
    cosm = np.tile(np.cos(ang_m), (4, 1))
    sgn = np.where((np.arange(128) % 64) < 32, -1.0, 1.0)[:, None]
    sinm = np.tile(np.sin(ang_m), (4, 1)) * sgn
    c_cs = np.stack([np.cos(ang_r), np.sin(ang_r), cosm, sinm], axis=1).astype(np.float32)

    logg = np.log1p(-np.exp2(-5.0 - np.arange(8, dtype=np.float64)))
    i = np.arange(128)[:, None].astype(np.float64)
    n = np.arange(512)[None, :].astype(np.float64)
    ci, cn = (i // 64), (n // 64)
    dmqd = np.zeros((8, 128, 2, 512), np.float64)
    for h in range(8):
        same = np.exp(logg[h] * np.abs(n - i))
        later = np.exp(logg[h] * (n - i))
        dm = np.where(cn == ci, same, np.where(cn > ci, later, 0.0)) / 16.0
        dmqd[h, :, 0, :] = dm
        dmqd[h, :, 1, :] = np.exp(logg[h] * (n + 1.0))
    small = np.zeros((128, 1024), np.float64)
    fm = lambda v: np.asarray(v, np.float64).reshape(-1, 128).T
    small[:, 0:32] = fm(g_attn); small[:, 32:64] = fm(g_ffn); small[:, 64:96] = fm(g_ple)
    small[:, 96:104] = fm(g_q); small[:, 104:108] = fm(g_kv)
    small[:, 108:140] = 0.0
    if half == 0:
        small[:, 108:108 + 16] = NEG
    pidx = np.arange(128, dtype=np.float64)
    for li, L in enumerate((512, 384, 128)):
        for h in range(8):
            for b in range(4):
                small[:, 140 + li * 32 + h * 4 + b] = np.exp(logg[h] * (L - 1 - 128 * b - pidx)) / 16.0
    cw = np.asarray(conv_w, np.float64)
    for j in range(3):
        small[:, 236 + j:236 + 3 * NFF:3] = fm(cw[j])
    small[:, 494:494 + NFF] = fm(conv_b)
    small[:, 580] = EPS
    gfin = np.broadcast_to(np.asarray(g_final, np.float32)[None, :], (128, D)).copy()
    return c_cs, dmqd.astype(np.float32), small.astype(np.float32), gfin


_CACHE = {}


def _core_inputs(c, x, pfull, weights, consts):
    b, half = c // 2, c % 2
    xo = np.ascontiguousarray(x[b, half * HALF:(half + 1) * HALF])
    xp = np.ascontiguousarray(x[b, 0:HALF]) if half == 1 else np.zeros((HALF, D), np.float32)
    m = {"xo": xo, "xp": xp, "pp": np.ascontiguousarray(pfull[0, b, half * HALF:(half + 1) * HALF])}
    m.update(weights)
    c_cs, dmqd, small, gfin = consts[half]
    m.update({"c_cs": c_cs, "c_dmqd": dmqd, "c_small": small, "c_gfin": gfin, "c_ident": np.eye(128, dtype=np.float32)})
    return m


def kernel(x, p, w_in, g_attn, g_q_lora, g_kv_lora, w_uq, w_ukv, w_o, g_ffn, w_ffn_gate, w_ffn_up, conv_w, conv_b,
           w_ffn_down, g_ple, w_ple_gate, w_ple_proj, g_final):
    f = lambda a: np.ascontiguousarray(np.asarray(a, dtype=np.float32))
    x = f(x); p = f(p)
    weights = {"w_in": f(w_in)[0], "w_uq": f(w_uq)[0], "w_ukv": f(w_ukv)[0], "w_o": f(w_o)[0], "w_fg": f(w_ffn_gate)[0],
               "w_fu": f(w_ffn_up)[0], "w_fd": f(w_ffn_down)[0], "w_pg": f(w_ple_gate)[0], "w_pp": f(w_ple_proj)[0]}
    consts = [_host_consts(h, f(g_attn)[0], f(g_ffn)[0], f(g_ple)[0], f(g_q_lora)[0], f(g_kv_lora)[0], f(conv_w)[0], f(conv_b)[0],
                           f(g_final)) for h in range(2)]
    if "nc" not in _CACHE:
        _CACHE["nc"] = build_program()
    nc = _CACHE["nc"]
    in_maps = [_core_inputs(c, x, p, weights, consts) for c in range(8)]
    res = run_bass_kernel_spmd(nc, in_maps, core_ids=list(range(8)))
    outp = np.empty((4, SEQ, D), np.float32)
    for c in range(8):
        b, half = c // 2, c % 2
        outp[b, half * HALF:(half + 1) * HALF] = np.asarray(res.results[c]["out"], dtype=np.float32)
    return outp
```

```python
import numpy as np
from contextlib import ExitStack
import concourse.bass as bass
import concourse.mybir as mybir
from concourse.bass_utils import run_bass_kernel_spmd

F32 = mybir.dt.float32
BF16 = mybir.dt.bfloat16
AF = mybir.ActivationFunctionType
ALU = mybir.AluOpType

D = 4096
SEQ = 4096
HALF = 2048
INW = 9792
OFF_RQ, OFF_RK, OFF_RV, OFF_RG, OFF_CQ, OFF_CKV, OFF_KR = 0, 2048, 4096, 6144, 8192, 9216, 9728
QL, KVL, DFF, PLED = 1024, 512, 11008, 256
NFF = DFF // 128
EPS = 1e-6
SCALE = 192.0 ** -0.5
NEG = -30000.0
TILES = [(0, 512, "pre"), (512, 512, "pre"), (1024, 512, "pre"), (1536, 512, "halo"),
         (2048, 512, "own"), (2560, 512, "own"), (3072, 512, "own"), (3584, 512, "own")]
LIDX = {512: 0, 384: 1, 128: 2}
ENGS = ("pe", "act", "dve", "pool", "sp")


class Tok:
    __slots__ = ("sem", "val")

    def __init__(self, sem, val):
        self.sem = sem
        self.val = val


class Buf:
    def __init__(self, ap, arena, lo, hi):
        self.ap = ap
        self.reg = (arena, lo, hi)

    def __getitem__(self, k):
        return self.ap[k]

    def sub(self, lo, hi):
        return (self.reg[0], self.reg[1] + lo, self.reg[1] + hi)


def _regs(lst):
    out = []
    for x in lst:
        if x is None:
            continue
        if isinstance(x, Buf):
            out.append(x.reg)
        elif isinstance(x, list):
            out.extend(_regs(x))
        else:
            out.append(x)
    return out


class Prog:
    def __init__(self, nc, es):
        self.nc = nc
        self.es = es
        self.q = {e: [] for e in ENGS}
        self.psem = {e: es.enter_context(nc.semaphore("prog_" + e)) for e in ENGS}
        self.pcnt = {e: 0 for e in ENGS}
        self.seen = {e: {} for e in ENGS}
        self.dma_sems = {}
        self.dma_cnt = {}
        self.ar = {}
        self.nops = 0

    def _deps(self, reads, writes):
        toks = []
        for (a, lo, hi) in reads:
            d = self.ar.get(a)
            if d:
                for (l2, h2), ent in d.items():
                    if l2 < hi and lo < h2 and ent[0] is not None:
                        toks.append(ent[0])
        for (a, lo, hi) in writes:
            d = self.ar.get(a)
            if d:
                for (l2, h2), ent in d.items():
                    if l2 < hi and lo < h2:
                        if ent[0] is not None:
                            toks.append(ent[0])
                        toks.extend(ent[1].values())
        return toks

    def _commit(self, tok, reads, writes):
        for (a, lo, hi) in writes:
            d = self.ar.setdefault(a, {})
            for k in [k for k in d if lo <= k[0] and k[1] <= hi]:
                del d[k]
            d[(lo, hi)] = [tok, {}]
        for (a, lo, hi) in reads:
            d = self.ar.setdefault(a, {})
            ent = d.setdefault((lo, hi), [None, {}])
            ent[1][id(tok.sem)] = tok

    def _waits(self, eng, toks):
        out = []
        for t in toks:
            if t is None:
                continue
            if eng == "pe" and t.sem is self.psem["pe"]:
                continue
            k = id(t.sem)
            if self.seen[eng].get(k, -1) >= t.val:
                continue
            self.seen[eng][k] = t.val
            out.append((t.sem, t.val))
        return out

    def op(self, eng, fn, r=(), w=(), extra=()):
        reads, writes = _regs(r), _regs(w)
        wt = self._waits(eng, self._deps(reads, writes) + list(extra))
        self.pcnt[eng] += 1
        tok = Tok(self.psem[eng], self.pcnt[eng])
        self.q[eng].append((wt, [fn], (self.psem[eng], 1)))
        self._commit(tok, reads, writes)
        self.nops += 1
        return tok

    def group(self, eng, fns, r=(), w=(), extra=()):
        reads, writes = _regs(r), _regs(w)
        wt = self._waits(eng, self._deps(reads, writes) + list(extra))
        self.pcnt[eng] += 1
        tok = Tok(self.psem[eng], self.pcnt[eng])
        self.q[eng].append((wt, list(fns), (self.psem[eng], 1)))
        self._commit(tok, reads, writes)
        self.nops += len(fns)
        return tok

    def dma(self, eng, slot, fns, r=(), w=(), extra=()):
        if slot not in self.dma_sems:
            self.dma_sems[slot] = self.es.enter_context(self.nc.semaphore("dma_" + slot))
            self.dma_cnt[slot] = 0
        sem = self.dma_sems[slot]
        reads, writes = _regs(r), _regs(w)
        prev = [Tok(sem, self.dma_cnt[slot])] if self.dma_cnt[slot] else []
        wt = self._waits(eng, self._deps(reads, writes) + list(extra) + prev)
        tok = None
        for i, fn in enumerate(fns):
            self.dma_cnt[slot] += 16
            self.q[eng].append((wt if i == 0 else [], [fn], (sem, 16)))
        tok = Tok(sem, self.dma_cnt[slot])
        self._commit(tok, reads, writes)
        return tok

    def wait_only(self, eng, toks):
        wt = self._waits(eng, toks)
        if wt:
            self.q[eng].append((wt, [], None))

    def emit(self):
        nc = self.nc
        with nc.Block() as block:
            def run(name):
                def body(e):
                    for (wt, fns, inc) in self.q[name]:
                        for (sem, val) in wt:
                            e.wait_ge(sem, val)
                        ins = None
                        for fn in fns:
                            ins = fn(e)
                        if inc is not None and ins is not None:
                            ins.then_inc(inc[0], inc[1])
                return body
            block.tensor(run("pe"))
            block.scalar(run("act"))
            block.vector(run("dve"))
            block.gpsimd(run("pool"))
            block.sync(run("sp"))


class PsumPool:
    def __init__(self, nc, es):
        self.t = [es.enter_context(nc.psum_tensor(f"psb{i}", [128, 512], F32)) for i in range(8)]
        self.bufs = [Buf(self.t[i], "psum", i * 512, (i + 1) * 512) for i in range(8)]
        for i, b in enumerate(self.bufs):
            b.idx = i
            b.bf = self.t[i].bitcast(BF16)
        self.held = [False] * 8
        self.nxt = 0

    def alloc(self):
        for _ in range(8):
            i = self.nxt
            self.nxt = (self.nxt + 1) % 8
            if not self.held[i]:
                self.held[i] = True
                return self.bufs[i]
        raise RuntimeError("PSUM exhausted")

    def free(self, b):
        if isinstance(b, list):
            for x in b:
                self.free(x)
            return
        assert self.held[b.idx]
        self.held[b.idx] = False


STATS = {'mm': 0, 'mm_cols': 0, 'mm_cols_f32': 0, 'tr': 0, 'tr_f32': 0}


class _Stop(Exception):
    pass


def build_program(tiles=TILES, dbg=None, stop=None):
    nc = bass.Bass("TRN2", target_bir_lowering=False)

    def din(name, shape, dt=F32):
        return nc.dram_tensor(name, list(shape), dt, kind="ExternalInput").ap()

    xo = din("xo", [HALF, D]); xp = din("xp", [HALF, D]); pp = din("pp", [HALF, PLED])
    w_in = din("w_in", [D, INW]); w_uq = din("w_uq", [QL, 3072]); w_ukv = din("w_ukv", [KVL, 4096])
    w_o = din("w_o", [D, D]); w_fg = din("w_fg", [D, DFF]); w_fu = din("w_fu", [D, DFF])
    w_fd = din("w_fd", [DFF, D]); w_pg = din("w_pg", [D, D]); w_pp = din("w_pp", [PLED, D])
    c_cs = din("c_cs", [128, 4, SEQ])
    c_dmqd = din("c_dmqd", [8, 128, 2, 512])
    c_small = din("c_small", [128, 1024])
    c_gfin = din("c_gfin", [128, D])
    c_ident = din("c_ident", [128, 128])
    out = nc.dram_tensor("out", [HALF, D], F32, kind="ExternalOutput").ap()
    kc = nc.dram_tensor("kc_scratch", [16, 128, SEQ], BF16, kind="Internal").ap()
    vc = nc.dram_tensor("vc_scratch", [8, SEQ, 256], BF16, kind="Internal").ap()
    dbg_out = {}
    if dbg:
        for name, shape in dbg.items():
            dbg_out[name] = nc.dram_tensor("dbg_" + name, list(shape), F32, kind="ExternalOutput").ap()

    with ExitStack() as es:
        es.enter_context(nc.allow_low_precision("bf16 matmul operands, fp32 accumulation"))
        p = Prog(nc, es)
        ps = PsumPool(nc, es)

        def sb(name, shape, dt):
            return es.enter_context(nc.sbuf_tensor(name, list(shape), dt))

        A_t = sb("arenaA", [128, 16384], BF16)
        B_t = sb("arenaB", [128, 8192], F32)
        C_t = sb("arenaC", [128, 16384], F32)
        D_t = sb("arenaD", [128, 2048], F32)
        krT_t = sb("krT", [128, SEQ], BF16)
        S_t = sb("Sst", [128, 8 * 512], F32)
        small_t = sb("small", [128, 1024], F32)
        carry_t = sb("carry", [128, NFF * 2], F32)
        stat_t = sb("stat", [128, 64], F32)
        identF_t = sb("identF", [128, 128], F32)
        identB_t = sb("identB", [128, 128], BF16)
        onesF_t = sb("onesF", [128, 128], F32)
        onesB_t = sb("onesB", [128, 128], BF16)
        NW = 5
        wr_t = [sb(f"wring{i}", [128, 4096], BF16) for i in range(NW)]

        def vw(t, arena, lo, hi, dt=F32, pat=None, **kw):
            if arena == "A":
                ap = t[:, lo:hi]
            elif dt == F32:
                ap = t[:, lo:hi]
            else:
                ap = t.bitcast(BF16)[:, 2 * lo:2 * hi]
            if pat:
                ap = ap.rearrange(pat, **kw)
            return Buf(ap, arena, lo, hi)

        hnT = vw(A_t, "A", 0, 16384, BF16, "p (c t) -> p c t", c=32)
        concatT = vw(B_t, "B", 0, 8192, BF16, "p (c t) -> p c t", c=32)
        junk = vw(B_t, "B", 4096, 6144, BF16)
        Cblk = [vw(C_t, "C", b * 4096, (b + 1) * 4096) for b in range(4)]
        stgb = [vw(D_t, "D", i * 512, (i + 1) * 512, BF16) for i in range(4)]
        krT = Buf(krT_t, "krT", 0, SEQ)
        Sst = [Buf(S_t[:, h * 512:(h + 1) * 512], "S", h * 512, (h + 1) * 512) for h in range(8)]
        small = Buf(small_t, "small", 0, 1024)
        carry = Buf(carry_t[:, :].rearrange("p (c t) -> p c t", t=2), "carry", 0, NFF * 2)
        stat = Buf(stat_t, "stat", 0, 64)
        identF = Buf(identF_t, "identF", 0, 1); identB = Buf(identB_t, "identB", 0, 1)
        onesF = Buf(onesF_t, "onesF", 0, 1); onesB = Buf(onesB_t, "onesB", 0, 1)
        G_A, G_F, G_P, G_Q, G_KV, KB, KDEC, CW, CB, EPSC = 0, 32, 64, 96, 104, 108, 140, 236, 494, 580
        eps_ap = small_t[:, EPSC:EPSC + 1]

        qT = vw(C_t, "C", 0, 1024, BF16, "p (c t) -> p c t", c=4)
        qdT = vw(C_t, "C", 1024, 2048, BF16, "p (c t) -> p c t", c=4)
        kT = vw(C_t, "C", 2048, 3072, BF16, "p (c t) -> p c t", c=4)
        Vb = vw(C_t, "C", 3072, 4096, BF16, "p (b c) -> p b c", b=4)
        Vd = vw(C_t, "C", 4096, 5120, BF16, "p (b c) -> p b c", b=4)
        sgb = vw(C_t, "C", 5120, 7168, F32, "p (c t) -> p c t", c=4)
        T1 = vw(C_t, "C", 7168, 7680); T2 = vw(C_t, "C", 7680, 8192); R1 = vw(C_t, "C", 8192, 8704)
        ktok = vw(C_t, "C", 8704, 9216, BF16)
        scD = [vw(C_t, "C", 9216 + 256 * i, 9216 + 256 * (i + 1), BF16) for i in range(4)]
        o32 = vw(C_t, "C", 10240, 11264, F32, "p (c t) -> p c t", c=2)
        sq32 = vw(C_t, "C", 11264, 12288, F32, "p (c t) -> p c t", c=2)
        mu = vw(C_t, "C", 12288, 12800); rs = vw(C_t, "C", 12800, 13312); tm = vw(C_t, "C", 13312, 13824)
        Sbf = vw(C_t, "C", 13824, 14080, BF16)
        cs = vw(C_t, "C", 14336, 16384, F32, "p (k t) -> p k t", k=4)
        dmqd = vw(D_t, "D", 0, 2048, F32, "p (h k t) -> p h k t", h=2, k=2)
        cq32 = vw(C_t, "C", 0, 4096, F32, "p (c t) -> p c t", c=8)
        ckv32 = vw(C_t, "C", 4096, 6144, F32, "p (c t) -> p c t", c=4)
        sq16 = vw(C_t, "C", 6144, 8192, BF16, "p (c t) -> p c t", c=8)
        cqnT = vw(C_t, "C", 8192, 10240, BF16, "p (c t) -> p c t", c=8)
        ckvnT = vw(C_t, "C", 10240, 11264, BF16, "p (c t) -> p c t", c=4)
        rstdq = vw(C_t, "C", 11264, 11776); rstdkv = vw(C_t, "C", 11776, 12288)
        kra = vw(C_t, "C", 12288, 12800); krb = vw(C_t, "C", 12800, 13312)
        T1m = vw(C_t, "C", 13312, 13824); T2m = vw(C_t, "C", 13824, 14336)
        Kst = vw(C_t, "C", 0, 4096, BF16, "p (h t) -> p h t", h=16)
        Vst = vw(C_t, "C", 4096, 8192, BF16, "p (b c) -> p b c", b=4)
        kTp = [vw(C_t, "C", 0, 2048, BF16, "p (h t) -> p h t", h=2), vw(C_t, "C", 2048, 4096, BF16, "p (h t) -> p h t", h=2)]
        vvp = [vw(C_t, "C", 4096, 6144, BF16, "p (b c) -> p b c", c=256), vw(C_t, "C", 6144, 8192, BF16, "p (b c) -> p b c", c=256)]
        qn = vw(C_t, "C", 10240, 10752, BF16, "p (h t) -> p h t", h=2)
        qra = vw(C_t, "C", 10752, 11264); qrb = vw(C_t, "C", 11264, 11776)
        qr = vw(C_t, "C", 11776, 12032, BF16)
        PT = [vw(C_t, "C", 12032 + 256 * i, 12032 + 256 * (i + 1), BF16) for i in range(3)]
        rz = vw(C_t, "C", 12800, 13312)
        Wr = vw(D_t, "D", 0, 512, BF16, "p (k c) -> p k c", k=8)
        qrp1 = vw(D_t, "D", 512, 768, BF16)
        Qs = None
        PT = PT + [vw(D_t, "D", 1792, 2048, BF16), vw(C_t, "C", 14080, 14336, BF16)]
        qn2 = vw(D_t, "D", 768, 1280, BF16, "p (h t) -> p h t", h=2)
        qr2 = vw(D_t, "D", 1280, 1536, BF16)
        qrp1b = vw(D_t, "D", 1536, 1792, BF16)
        Qs = [(qn, qr, qrp1), (qn2, qr2, qrp1b)]
        actT = [vw(B_t, "B", 2048 * i, 2048 * (i + 1), BF16, "p (c t) -> p c t", c=8) for i in range(2)]
        NGX = 6
        gext = [vw(B_t, "B", 4096 + 514 * i, 4096 + 514 * (i + 1)) for i in range(NGX)]
        ft = [vw(D_t, "D", 512 * i, 512 * (i + 1)) for i in range(4)]
        gfin = vw(B_t, "B", 0, 4096)
        pbuf = vw(B_t, "B", 6144, 7168, F32, "p (b c) -> p b c", b=4)
        pT = vw(B_t, "B", 7168, 7680, BF16, "p (c t) -> p c t", c=2)
        ppo = [vw(D_t, "D", 512 * i, 512 * (i + 1)) for i in range(4)]
        sgm = [vw(B_t, "B", 7680 + 256 * 0, 7680 + 256 * 0 + 512)]
        sgm = [vw(C_t, "C", 0, 0)]
        sgt = [vw(B_t, "B", 4096, 4608), vw(B_t, "B", 4608, 5120)]

        def ACT(out_, in_, func, r, w, bias=None, scale=None, accum=None):
            kw = {}
            if bias is not None:
                kw["bias"] = bias
            if scale is not None:
                kw["scale"] = scale
            if accum is not None:
                kw["accum_out"] = accum
            return p.op("act", lambda e: e.activation(out=out_, in_=in_, func=func, **kw), r, w)

        def TT(out_, in0, in1, op, r, w, eng="dve"):
            return p.op(eng, lambda e: e.tensor_tensor(out=out_, in0=in0, in1=in1, op=op), r, w)

        def TS(out_, in0, s1, s2, op0, op1, r, w, eng="dve"):
            if op1 is None:
                return p.op(eng, lambda e: e.tensor_scalar(out=out_, in0=in0, scalar1=s1, scalar2=None, op0=op0), r, w)
            return p.op(eng, lambda e: e.tensor_scalar(out=out_, in0=in0, scalar1=s1, scalar2=s2, op0=op0, op1=op1), r, w)

        def STT(out_, in0, sc, in1, op0, op1, r, w, accum=None, eng="dve"):
            kw = {}
            if accum is not None:
                kw["accum_out"] = accum
            return p.op(eng, lambda e: e.scalar_tensor_tensor(out=out_, in0=in0, scalar=sc, in1=in1, op0=op0, op1=op1, **kw), r, w)

        def CP(out_, in_, r, w, eng="dve"):
            return p.op(eng, lambda e: e.tensor_copy(out=out_, in_=in_), r, w)

        def RCP(out_, in_, r, w):
            return p.op("dve", lambda e: e.reciprocal(out=out_, in_=in_), r, w)

        def MM(out_, lhsT, rhs, start, stop):
            STATS['mm'] += 1
            STATS['mm_cols'] += out_.shape[-1]
            if lhsT.dtype == F32:
                STATS['mm_cols_f32'] += out_.shape[-1]
            return lambda e: e.matmul(out_, lhsT, rhs, start=start, stop=stop)

        def TR(out_, in_, ident):
            STATS['tr'] += 1
            if in_.dtype == F32:
                STATS['tr_f32'] += 1
            return lambda e: e.transpose(out_, in_, ident)

        wstate = {"n": 0}

        def wload(parts, shape):
            i = wstate["n"] % NW
            wstate["n"] += 1
            t = wr_t[i]
            nk, ncol = shape
            view = t[:, 0:nk * ncol].rearrange("p (k c) -> p k c", k=nk)
            buf = Buf(view, "wring", i * 4096, (i + 1) * 4096)
            fns = []
            for (cs_, src) in parts:
                dst = view if cs_ is None else view[:, :, cs_[0]:cs_[1]]
                fns.append(lambda e, dst=dst, src=src: e.dma_start(out=dst, in_=src))
            p.dma("pool", f"w{i}", fns, r=[], w=[buf])
            return buf

        def wslab(w, k0, nk, c0, ncol):
            src = w.rearrange("(kc p) n -> p kc n", p=128)[:, k0:k0 + nk, c0:c0 + ncol]
            return wload([(None, src)], (nk, ncol))

        def fm_proj(w, c0, ncol, nkc, rhs_fn, rhs_regs, N, loader=None, extra=None):
            nch = ncol // 128
            banks = [ps.alloc() for _ in range(nch)]
            bx = ps.alloc() if extra else None
            for si, s0 in enumerate(range(0, nkc, 8)):
                sk = min(8, nkc - s0)
                slab = loader(s0, sk) if loader else wslab(w, s0, sk, c0, ncol)
                fns = []
                for j in range(nch):
                    for k in range(sk):
                        fns.append(MM(banks[j][:, 0:N], slab[:, k, j * 128:(j + 1) * 128], rhs_fn(s0 + k),
                                      s0 + k == 0, s0 + k == nkc - 1))
                if extra:
                    n2 = extra[2]
                    for j in range(nch):
                        col = (si * nch + j) * n2
                        for k in range(sk):
                            fns.append(MM(bx[:, col:col + n2], slab[:, k, j * 128:(j + 1) * 128], extra[0](s0 + k), k == 0, k == sk - 1))
                p.group("pe", fns, r=[slab] + rhs_regs + (extra[1] if extra else []), w=banks + ([bx] if extra else []))
            if extra:
                banks.append(bx)
            return banks

        def tm_proj(w, c0, ncol, k0, nkc, lhs_fn, lhs_regs, nb):
            banks = [ps.alloc() for _ in range(nb)]
            for s0 in range(0, nkc, 8):
                sk = min(8, nkc - s0)
                slab = wslab(w, k0 + s0, sk, c0, ncol)
                fns = []
                for b in range(nb):
                    for k in range(sk):
                        fns.append(MM(banks[b][:, 0:ncol], lhs_fn(s0 + k, b), slab[:, k, 0:ncol],
                                      s0 + k == 0, s0 + k == nkc - 1))
                p.group("pe", fns, r=[slab] + lhs_regs, w=banks)
            return banks

        p.dma("sp", "c0", [lambda e: e.dma_start(out=small_t[:, :], in_=c_small)], w=[small])
        p.dma("sp", "c1", [lambda e: e.dma_start(out=identF_t[:, :], in_=c_ident)], w=[identF])
        CP(identB_t[:, :], identF_t[:, :], [identF], [identB])
        p.op("dve", lambda e: e.memset(onesF_t[:, :], 1.0), [], [onesF])
        p.op("dve", lambda e: e.memset(onesB_t[:, :], 1.0), [], [onesB])
        p.op("dve", lambda e: e.memset(S_t[:, :], 0.0), [], Sst)
        p.op("dve", lambda e: e.memset(carry_t[:, :], 0.0), [], [carry])

        def norm_T(nb, gcol, blocks=None):
            it = 0
            for b in (blocks if blocks is not None else range(nb)):
                ACT(junk.ap, Cblk[b].ap, AF.Square, [Cblk[b]], [junk, stat.sub(b, b + 1)], accum=stat_t[:, b:b + 1])
                ACT(stat_t[:, 8 + b:9 + b], stat_t[:, b:b + 1], AF.Sqrt, [stat.sub(b, b + 1), small], [stat.sub(8 + b, 9 + b)],
                    bias=eps_ap, scale=1.0 / D)
                RCP(stat_t[:, 8 + b:9 + b], stat_t[:, 8 + b:9 + b], [stat.sub(8 + b, 9 + b)], [stat.sub(8 + b, 9 + b)])
                for q in range(4):
                    s = stgb[it % 4]
                    it += 1
                    ACT(s.ap, Cblk[b][:, q * 1024:(q + 1) * 1024], AF.Copy, [Cblk[b], stat.sub(8 + b, 9 + b)], [s],
                        scale=stat_t[:, 8 + b:9 + b])
                    bk = ps.alloc()
                    fns = [TR(bk.bf[:, i * 128:(i + 1) * 128], s[:, i * 128:(i + 1) * 128], identB_t[:, :]) for i in range(8)]
                    p.group("pe", fns, r=[s, identB], w=[bk])
                    c0 = q * 8
                    gb = small_t[:, gcol + c0:gcol + c0 + 8].unsqueeze(2).broadcast_to([128, 8, 128])
                    TT(hnT[:, c0:c0 + 8, b * 128:(b + 1) * 128], bk.bf[:, 0:1024].rearrange("p (c t) -> p c t", c=8), gb,
                       ALU.mult, [bk, small], [hnT])
                    ps.free(bk)

        def load_x(ltok0, L):
            nb = L // 128
            for b in range(nb):
                t0_ = ltok0 + b * 128
                src = (xp[t0_:t0_ + 128, :] if t0_ < HALF else xo[t0_ - HALF:t0_ - HALF + 128, :])
                p.dma("sp", f"xld{b}", [lambda e, src=src, b=b: e.dma_start(out=Cblk[b].ap, in_=src)], r=[], w=[Cblk[b]])

        def rope_pair(banks, N, outT, qd, toff):
            cosr, sinr = cs[:, 0, toff:toff + N], cs[:, 1, toff:toff + N]
            for hi in range(2):
                x1, x2 = banks[2 * hi], banks[2 * hi + 1]
                for half in range(2):
                    a, b_ = (x1, x2) if half == 0 else (x2, x1)
                    TT(T1[:, 0:N], a[:, 0:N], cosr, ALU.mult, [a, cs], [T1])
                    TT(T2[:, 0:N], b_[:, 0:N], sinr, ALU.mult, [b_, cs], [T2])
                    op = ALU.subtract if half == 0 else ALU.add
                    if qd is None:
                        TT(outT[:, 2 * hi + half, 0:N], T1[:, 0:N], T2[:, 0:N], op, [T1, T2], [outT])
                    else:
                        TT(R1[:, 0:N], T1[:, 0:N], T2[:, 0:N], op, [T1, T2], [R1])
                        ACT(outT[:, 2 * hi + half, 0:N], R1[:, 0:N], AF.Copy, [R1], [outT])
                        TT(qd[:, 2 * hi + half, 0:N], R1[:, 0:N], dmqd[:, hi, 1, 0:N], ALU.mult, [R1, dmqd], [qd])

        def hn_rhs(N0, N1):
            return lambda k: hnT[:, k, N0:N1]

        hhalo_t = sb("hhalo", [128, 64], BF16)
        hhalo = Buf(hhalo_t[:, :].rearrange("p (c t) -> p c t", t=2), "hhalo", 0, 64)

        def chk(k):
            if stop == k:
                raise _Stop()

        first_own = True
        for (ltok0, L, mode) in tiles:
          try:
            nb = L // 128
            hasq = mode != "pre"
            q0 = 384 if mode == "halo" else 0
            Lq = (L - q0) if hasq else 0
            qb0 = q0 // 128
            nbq = Lq // 128
            load_x(ltok0, L)
            norm_T(nb, G_A)
            p.dma("sp", "csld", [lambda e, l0=ltok0, L=L: e.dma_start(out=cs[:, :, 0:L], in_=c_cs[:, :, l0:l0 + L])], r=[], w=[cs])

            chk(1)
            for hp in range(4):
                h0 = 2 * hp
                if hasq:
                    p.dma("sp", "dmqd", [lambda e, h0=h0: e.dma_start(out=dmqd.ap, in_=c_dmqd[h0:h0 + 2].rearrange("h p k t -> p h k t"))],
                          r=[], w=[dmqd])
                bk_ = fm_proj(w_in, OFF_RK + 512 * hp, 512, 32, hn_rhs(0, L), [hnT], L)
                rope_pair(bk_, L, kT, None, 0)
                ps.free(bk_)
                if hasq:
                    bq_ = fm_proj(w_in, OFF_RQ + 512 * hp, 512, 32, hn_rhs(q0, L), [hnT], Lq)
                    rope_pair(bq_, Lq, qT, qdT, q0)
                    ps.free(bq_)
                bv_ = tm_proj(w_in, OFF_RV + 512 * hp, 512, 0, 32, lambda k, b: hnT[:, k, b * 128:(b + 1) * 128], [hnT], nb)
                for b in range(nb):
                    if hasq and b >= qb0:
                        ACT(Vb[:, b - qb0, :], bv_[b][:, :], AF.Copy, [bv_[b]], [Vb])
                    if b < qb0:
                        lidx, bb = LIDX[q0], b
                    else:
                        lidx, bb = LIDX[L - q0], b - qb0
                    for hi in range(2):
                        col = KDEC + lidx * 32 + (h0 + hi) * 4 + bb
                        ACT(Vd[:, b, hi * 256:(hi + 1) * 256], bv_[b][:, hi * 256:(hi + 1) * 256], AF.Copy, [bv_[b], small], [Vd],
                            scale=small_t[:, col:col + 1])
                ps.free(bv_)
                if hasq:
                    bg_ = fm_proj(w_in, OFF_RG + 512 * hp, 512, 32, hn_rhs(q0, L), [hnT], Lq)
                    for j in range(4):
                        ACT(sgb[:, j, 0:Lq], bg_[j][:, 0:Lq], AF.Silu, [bg_[j]], [sgb])
                    ps.free(bg_)
                for hi in range(2):
                    h = h0 + hi
                    bk = ps.alloc()
                    fns = [TR(bk.bf[:, b * 256 + dc * 128:b * 256 + (dc + 1) * 128], kT[:, 2 * hi + dc, b * 128:(b + 1) * 128], identB_t[:, :])
                           for b in range(nb) for dc in range(2)]
                    p.group("pe", fns, r=[kT, identB], w=[bk])
                    ACT(ktok[:, 0:nb * 256], bk.bf[:, 0:nb * 256], AF.Copy, [bk], [ktok])
                    ps.free(bk)

                    def state_update(b_lo, b_hi, ntok, h=h, hi=hi):
                        bu = ps.alloc()
                        fns = []
                        for dc in range(2):
                            for b in range(b_lo, b_hi):
                                fns.append(MM(bu[:, dc * 256:(dc + 1) * 256], ktok[:, b * 256 + dc * 128:b * 256 + (dc + 1) * 128],
                                              Vd[:, b, hi * 256:(hi + 1) * 256], b == b_lo, b == b_hi - 1))
                        p.group("pe", fns, r=[ktok, Vd], w=[bu])
                        sdec = float(np.exp(np.log1p(-2.0 ** (-5.0 - h)) * ntok))
                        STT(Sst[h].ap, Sst[h].ap, sdec, bu[:, :], ALU.mult, ALU.add, [Sst[h], bu], [Sst[h]])
                        ps.free(bu)

                    if qb0 > 0:
                        state_update(0, qb0, q0)
                    if hasq:
                        ACT(Sbf.ap, Sst[h].ap, AF.Copy, [Sst[h]], [Sbf])
                        for kb in range(nbq):
                            N = Lq - 128 * kb
                            bs = ps.alloc()
                            fns = [MM(bs[:, 0:N], kT[:, 2 * hi + dc, q0 + kb * 128:q0 + (kb + 1) * 128], qT[:, 2 * hi + dc, 128 * kb:Lq], dc == 0, dc == 1)
                                   for dc in range(2)]
                            p.group("pe", fns, r=[kT, qT], w=[bs])
                            TT(scD[kb][:, 0:N], bs[:, 0:N], dmqd[:, hi, 0, 0:N], ALU.mult, [bs, dmqd], [scD[kb]])
                            ps.free(bs)
                        for vcx in range(2):
                            bo = ps.alloc()
                            fns = [MM(bo[:, 0:Lq], Sbf[:, dc * 256 + vcx * 128:dc * 256 + (vcx + 1) * 128], qdT[:, 2 * hi + dc, 0:Lq], dc == 0, False)
                                   for dc in range(2)]
                            for kb in range(nbq):
                                N = Lq - 128 * kb
                                fns.append(MM(bo[:, 128 * kb:Lq], Vb[:, kb, hi * 256 + vcx * 128:hi * 256 + (vcx + 1) * 128], scD[kb][:, 0:N],
                                              False, kb == nbq - 1))
                            p.group("pe", fns, r=[Sbf, qdT, Vb] + scD[0:nbq], w=[bo])
                            ACT(o32[:, vcx, 0:Lq], bo[:, 0:Lq], AF.Copy, [bo], [o32])
                            ACT(sq32[:, vcx, 0:Lq], bo[:, 0:Lq], AF.Square, [bo], [sq32])
                            ps.free(bo)
                        bm = ps.alloc()
                        p.group("pe", [MM(bm[:, 0:Lq], onesF_t[:, :], o32[:, v, 0:Lq], v == 0, v == 1) for v in range(2)], r=[onesF, o32], w=[bm])
                        bq2 = ps.alloc()
                        p.group("pe", [MM(bq2[:, 0:Lq], onesF_t[:, :], sq32[:, v, 0:Lq], v == 0, v == 1) for v in range(2)], r=[onesF, sq32], w=[bq2])
                        ACT(mu[:, 0:Lq], bm[:, 0:Lq], AF.Copy, [bm], [mu], scale=1.0 / 256)
                        ps.free(bm)
                        TT(tm[:, 0:Lq], mu[:, 0:Lq], mu[:, 0:Lq], ALU.mult, [mu], [tm])
                        STT(rs[:, 0:Lq], bq2[:, 0:Lq], 1.0 / 256, tm[:, 0:Lq], ALU.mult, ALU.subtract, [bq2, tm], [rs])
                        ps.free(bq2)
                        ACT(rs[:, 0:Lq], rs[:, 0:Lq], AF.Ln, [rs, small], [rs], bias=eps_ap, scale=1.0)
                        ACT(rs[:, 0:Lq], rs[:, 0:Lq], AF.Exp, [rs], [rs], scale=-0.5)
                        for vcx in range(2):
                            TT(tm[:, 0:Lq], o32[:, vcx, 0:Lq], mu[:, 0:Lq], ALU.subtract, [o32, mu], [tm])
                            TT(tm[:, 0:Lq], tm[:, 0:Lq], rs[:, 0:Lq], ALU.mult, [tm, rs], [tm])
                            TT(concatT[:, 2 * h + vcx, 0:Lq], tm[:, 0:Lq], sgb[:, 2 * hi + vcx, 0:Lq], ALU.mult, [tm, sgb], [concatT])
                    state_update(qb0, nb, L - q0)

            chk(2)
            bc_ = fm_proj(w_in, OFF_CKV, 512, 32, hn_rhs(0, L), [hnT], L)
            for j in range(4):
                ACT(ckv32[:, j, 0:L], bc_[j][:, 0:L], AF.Copy, [bc_[j]], [ckv32])
                ACT(sq16[:, j, 0:L], bc_[j][:, 0:L], AF.Square, [bc_[j]], [sq16])
            ps.free(bc_)
            bst = ps.alloc()
            p.group("pe", [MM(bst[:, 0:L], onesB_t[:, :], sq16[:, j, 0:L], j == 0, j == 3) for j in range(4)], r=[onesB, sq16], w=[bst])
            ACT(rstdkv[:, 0:L], bst[:, 0:L], AF.Ln, [bst, small], [rstdkv], bias=eps_ap, scale=1.0 / KVL)
            ps.free(bst)
            ACT(rstdkv[:, 0:L], rstdkv[:, 0:L], AF.Exp, [rstdkv], [rstdkv], scale=-0.5)
            for j in range(4):
                STT(ckvnT[:, j, 0:L], ckv32[:, j, 0:L], small_t[:, G_KV + j:G_KV + j + 1], rstdkv[:, 0:L], ALU.mult, ALU.mult,
                    [ckv32, small, rstdkv], [ckvnT])
            w_kr = w_in.rearrange("(kc p) n -> p kc n", p=128)

            def kr_loader(s0, sk):
                src = w_kr[:, s0:s0 + sk, OFF_KR:OFF_KR + 64]
                return wload([((0, 64), src), ((64, 128), src)], (sk, 128))

            bkr = fm_proj(None, 0, 128, 32, hn_rhs(0, L), [hnT], L, loader=kr_loader)
            ACT(kra[:, 0:L], bkr[0][:, 0:L], AF.Copy, [bkr[0]], [kra])
            ps.free(bkr)
            for (d0, s0_) in ((0, 32), (32, 0), (64, 96), (96, 64)):
                CP(krb.ap[d0:d0 + 32, 0:L], kra.ap[s0_:s0_ + 32, 0:L], [kra], [krb])
            TT(T1m[:, 0:L], kra[:, 0:L], cs[:, 2, 0:L], ALU.mult, [kra, cs], [T1m])
            TT(T2m[:, 0:L], krb[:, 0:L], cs[:, 3, 0:L], ALU.mult, [krb, cs], [T2m])
            TT(krT_t[:, ltok0:ltok0 + L], T1m[:, 0:L], T2m[:, 0:L], ALU.add, [T1m, T2m], [krT.sub(ltok0, ltok0 + L)])
            if hasq:
                for g in range(2):
                    bcq = fm_proj(w_in, OFF_CQ + 512 * g, 512, 32, hn_rhs(q0, L), [hnT], Lq)
                    for j in range(4):
                        ACT(cq32[:, 4 * g + j, 0:Lq], bcq[j][:, 0:Lq], AF.Copy, [bcq[j]], [cq32])
                        ACT(sq16[:, 4 * g + j, 0:Lq], bcq[j][:, 0:Lq], AF.Square, [bcq[j]], [sq16])
                    ps.free(bcq)
                bst = ps.alloc()
                p.group("pe", [MM(bst[:, 0:Lq], onesB_t[:, :], sq16[:, j, 0:Lq], j == 0, j == 7) for j in range(8)], r=[onesB, sq16], w=[bst])
                ACT(rstdq[:, 0:Lq], bst[:, 0:Lq], AF.Ln, [bst, small], [rstdq], bias=eps_ap, scale=1.0 / QL)
                ps.free(bst)
                ACT(rstdq[:, 0:Lq], rstdq[:, 0:Lq], AF.Exp, [rstdq], [rstdq], scale=-0.5)
                for j in range(8):
                    STT(cqnT[:, j, 0:Lq], cq32[:, j, 0:Lq], small_t[:, G_Q + j:G_Q + j + 1], rstdq[:, 0:Lq], ALU.mult, ALU.mult,
                        [cq32, small, rstdq], [cqnT])

            chk(3)
            for s in range(4):
                slab = wslab(w_ukv, 0, 4, 1024 * s, 1024)
                for hh in range(4):
                    bk = ps.alloc()
                    p.group("pe", [MM(bk[:, 0:L], slab[:, k, hh * 256:hh * 256 + 128], ckvnT[:, k, 0:L], k == 0, k == 3) for k in range(4)],
                            r=[slab, ckvnT], w=[bk])
                    ACT(Kst[:, 4 * s + hh, 0:L], bk[:, 0:L], AF.Copy, [bk], [Kst])
                    ps.free(bk)
                for b in range(nb):
                    bv = ps.alloc()
                    fns = []
                    for hh in range(4):
                        for k in range(4):
                            fns.append(MM(bv[:, hh * 128:(hh + 1) * 128], ckvnT[:, k, b * 128:(b + 1) * 128],
                                          slab[:, k, hh * 256 + 128:hh * 256 + 256], k == 0, k == 3))
                    p.group("pe", fns, r=[slab, ckvnT], w=[bv])
                    CP(Vst[:, b, 512 * s:512 * (s + 1)], bv[:, :], [bv], [Vst])
                    ps.free(bv)
            kreg, vreg = ("kc", 0, 1), ("vc", 0, 1)
            p.dma("sp", "kvst", [lambda e, l0=ltok0, L=L: e.dma_start(out=kc[:, :, l0:l0 + L].rearrange("h d t -> d h t"), in_=Kst[:, :, 0:L])],
                  r=[Kst], w=[kreg])
            fns = []
            for b in range(nb):
                fns.append(lambda e, b=b, l0=ltok0: e.dma_start(
                    out=vc[:, l0 + b * 128:l0 + (b + 1) * 128, :].rearrange("q p c -> p q c"),
                    in_=Vst[:, b, :].rearrange("p (q c) -> p q c", q=8)))
            p.dma("sp", "kvst2", fns, r=[Vst], w=[vreg])

            chk(4)
            if not hasq:
                continue

            nkb = (ltok0 + L) // 128
            kh = nkb // 2
            dblk0 = (ltok0 + q0) // 128
            pti = 0

            def qproj(hp_):
                qn_b, qr_b, qrp_b = Qs[hp_ % 2]
                slab = wslab(w_uq, 0, 8, 384 * hp_, 384)
                CP(Wr[:, :, 0:64], slab[:, :, 128:192], [slab], [Wr])
                CP(Wr[:, :, 64:128], slab[:, :, 320:384], [slab], [Wr])
                for hi in range(2):
                    bqn = ps.alloc()
                    p.group("pe", [MM(bqn[:, 0:Lq], slab[:, k, hi * 192:hi * 192 + 128], cqnT[:, k, 0:Lq], k == 0, k == 7) for k in range(8)],
                            r=[slab, cqnT], w=[bqn])
                    ACT(qn_b[:, hi, 0:Lq], bqn[:, 0:Lq], AF.Copy, [bqn], [qn_b])
                    ps.free(bqn)
                bqr = ps.alloc()
                p.group("pe", [MM(bqr[:, 0:Lq], Wr[:, k, :], cqnT[:, k, 0:Lq], k == 0, k == 7) for k in range(8)], r=[Wr, cqnT], w=[bqr])
                ACT(qra[:, 0:Lq], bqr[:, 0:Lq], AF.Copy, [bqr], [qra])
                ps.free(bqr)
                for (d0, s0_) in ((0, 32), (32, 0), (64, 96), (96, 64)):
                    CP(qrb.ap[d0:d0 + 32, 0:Lq], qra.ap[s0_:s0_ + 32, 0:Lq], [qra], [qrb])
                TT(T1m[:, 0:Lq], qra[:, 0:Lq], cs[:, 2, q0:q0 + Lq], ALU.mult, [qra, cs], [T1m])
                TT(T2m[:, 0:Lq], qrb[:, 0:Lq], cs[:, 3, q0:q0 + Lq], ALU.mult, [qrb, cs], [T2m])
                TT(qr_b[:, 0:Lq], T1m[:, 0:Lq], T2m[:, 0:Lq], ALU.add, [T1m, T2m], [qr_b])
                p.op("dve", lambda e: e.memset(qrp_b.ap[0:64, 0:Lq], 0.0), [], [qrp_b])
                CP(qrp_b.ap[64:128, 0:Lq], qr_b.ap[64:128, 0:Lq], [qr_b], [qrp_b])
                p.op("dve", lambda e: e.memset(qr_b.ap[64:128, 0:Lq], 0.0), [qrp_b], [qr_b])

            qproj(0)
            for hp in range(8):
                for half, (b0, b1) in enumerate(((0, kh), (kh, nkb))):
                    nk_ = (b1 - b0) * 128
                    p.dma("sp", f"kld{half}", [lambda e, hp=hp, b0=b0, nk_=nk_, half=half: e.dma_start(
                        out=kTp[half][:, :, 0:nk_], in_=kc[2 * hp:2 * hp + 2, :, b0 * 128:b0 * 128 + nk_].rearrange("h d t -> d h t"))],
                        r=[kreg], w=[kTp[half]])
                    p.dma("sp", f"vld{half}", [lambda e, hp=hp, b0=b0, b1=b1, half=half: e.dma_start(
                        out=vvp[half][:, 0:b1 - b0, :], in_=vc[hp, b0 * 128:b1 * 128, :].rearrange("(b p) c -> p b c", p=128))],
                        r=[vreg], w=[vvp[half]])
                qn_c, qr_c, qrp_c = Qs[hp % 2]
                if hp < 7:
                    qproj(hp + 1)
                bo2 = [ps.alloc(), ps.alloc()]
                bz2 = [ps.alloc(), ps.alloc()]

                def scores(kb, hi):
                    nonlocal pti
                    half = 0 if kb < kh else 1
                    kl = kb - (0 if half == 0 else kh)
                    diag = kb >= dblk0
                    qlo = 128 * (kb - dblk0) if diag else 0
                    N = Lq - qlo
                    bs = ps.alloc()
                    fns = [MM(bs[:, 0:N], kTp[half][:, hi, kl * 128:(kl + 1) * 128], qn_c[:, hi, qlo:Lq], True, False),
                           MM(bs[:, 0:N], krT_t[:, kb * 128:(kb + 1) * 128], (qr_c if hi == 0 else qrp_c)[:, qlo:Lq], False, True)]
                    p.group("pe", fns, r=[kTp[half], qn_c, krT.sub(kb * 128, (kb + 1) * 128), qr_c, qrp_c], w=[bs])
                    pt = PT[pti % len(PT)]
                    pti += 1
                    if diag:
                        ACT(pt[:, 0:N], bs[:, 0:N], AF.Exp, [bs], [pt], scale=SCALE)
                        p.op("dve", lambda e, pt=pt: e.memset(pt.ap[64:128, 0:64], 0.0), [], [pt])
                    else:
                        ACT(pt[:, 0:N], bs[:, 0:N], AF.Exp, [bs, small], [pt], scale=SCALE, bias=small_t[:, KB + kb:KB + kb + 1])
                    ps.free(bs)
                    return (pt, N, qlo, half, kl)

                nxt = [scores(0, 0), scores(0, 1)]
                for kb in range(nkb):
                    cur = nxt
                    if kb + 1 < nkb:
                        nxt = [scores(kb + 1, 0), scores(kb + 1, 1)]
                    for hi in range(2):
                        pt, N, qlo, half, kl = cur[hi]
                        p.group("pe", [MM(bo2[hi][:, qlo:Lq], vvp[half][:, kl, hi * 128:(hi + 1) * 128], pt[:, 0:N], kb == 0, kb == nkb - 1),
                                       MM(bz2[hi][:, qlo:Lq], onesB_t[:, :], pt[:, 0:N], kb == 0, kb == nkb - 1)],
                                r=[vvp[half], pt, onesB], w=[bo2[hi], bz2[hi]])
                rzb = [rz, T2m]
                for hi in range(2):
                    ACT(rzb[hi][:, 0:Lq], bz2[hi][:, 0:Lq], AF.Ln, [bz2[hi]], [rzb[hi]])
                    ps.free(bz2[hi])
                for hi in range(2):
                    ACT(rzb[hi][:, 0:Lq], rzb[hi][:, 0:Lq], AF.Exp, [rzb[hi]], [rzb[hi]], scale=-1.0)
                for hi in range(2):
                    h = 2 * hp + hi
                    TT(concatT[:, 16 + h, 0:Lq], bo2[hi][:, 0:Lq], rzb[hi][:, 0:Lq], ALU.mult, [bo2[hi], rzb[hi]], [concatT])
                    ps.free(bo2[hi])

            chk(5)
            load_x(ltok0 + q0, Lq)
            for n in range(8):
                bw = tm_proj(w_o, 512 * n, 512, 0, 32, lambda k, b: concatT[:, k, b * 128:(b + 1) * 128], [concatT], nbq)
                for b in range(nbq):
                    reg = Cblk[b].sub(512 * n, 512 * (n + 1))
                    TT(Cblk[b][:, 512 * n:512 * (n + 1)], bw[b][:, :], Cblk[b][:, 512 * n:512 * (n + 1)], ALU.add, [bw[b], reg], [reg])
                ps.free(bw)
            if dbg and "h1" in dbg and mode == "own" and ltok0 == HALF:
                p.dma("sp", "dbg", [lambda e: e.dma_start(out=dbg_out["h1"], in_=Cblk[0].ap)], r=[Cblk[0]], w=[])

            chk(6)
            norm_T(nbq, G_F)
            if mode == "halo":
                CP(hhalo.ap, hnT[:, :, Lq - 2:Lq], [hnT], [hhalo])
                continue
            use_halo = first_own
            first_own = False
            gxs = {"i": 0}

            def GU(sgi):
                sg0 = sgi * 8
                nsg = min(8, NFF - sg0)
                aT = actT[sgi % 2]
                for g0 in range(sg0, sg0 + nsg, 4):
                    ng = min(4, sg0 + nsg - g0)
                    extra = (lambda k: hhalo[:, k, :], [hhalo], 2) if use_halo else None
                    gb_ = fm_proj(w_fg, g0 * 128, ng * 128, 32, hn_rhs(0, L), [hnT], L, extra=extra)
                    bx = gb_.pop() if use_halo else None
                    ges = []
                    for j in range(ng):
                        c = g0 + j
                        ge = gext[gxs["i"] % NGX]
                        gxs["i"] += 1
                        ges.append(ge)
                        if use_halo:
                            tb = stat.sub(16 + 8 * (j % 2), 24 + 8 * (j % 2))
                            t0c = 16 + 8 * (j % 2)
                            for si in range(4):
                                col = (si * ng + j) * 2
                                ACT(stat_t[:, t0c + 2 * si:t0c + 2 * si + 2], bx[:, col:col + 2], AF.Copy, [bx], [tb])
                            TT(ge[:, 0:2], stat_t[:, t0c:t0c + 2], stat_t[:, t0c + 2:t0c + 4], ALU.add, [tb], [ge])
                            TT(ge[:, 0:2], ge[:, 0:2], stat_t[:, t0c + 4:t0c + 6], ALU.add, [tb, ge], [ge])
                            TT(ge[:, 0:2], ge[:, 0:2], stat_t[:, t0c + 6:t0c + 8], ALU.add, [tb, ge], [ge])
                        else:
                            ACT(ge[:, 0:2], carry[:, c, :], AF.Copy, [carry], [ge])
                        ACT(ge[:, 2:2 + L], gb_[j][:, 0:L], AF.Copy, [gb_[j]], [ge])
                        ACT(carry[:, c, :], ge[:, L:L + 2], AF.Copy, [ge], [carry])
                    ps.free(gb_)
                    if bx is not None:
                        ps.free(bx)
                    for j in range(ng):
                        c = g0 + j
                        ge = ges[j]
                        t1 = ft[j]
                        cw = lambda i, c=c: small_t[:, CW + 3 * c + i:CW + 3 * c + i + 1]
                        TS(t1[:, 0:L], ge[:, 0:L], cw(0), small_t[:, CB + c:CB + c + 1], ALU.mult, ALU.add, [ge, small], [t1])
                        STT(t1[:, 0:L], ge[:, 1:L + 1], cw(1), t1[:, 0:L], ALU.mult, ALU.add, [ge, small, t1], [t1])
                        STT(t1[:, 0:L], ge[:, 2:L + 2], cw(2), t1[:, 0:L], ALU.mult, ALU.add, [ge, small, t1], [t1])
                        ACT(t1[:, 0:L], t1[:, 0:L], AF.Silu, [t1], [t1])
                    ub_ = fm_proj(w_fu, g0 * 128, ng * 128, 32, hn_rhs(0, L), [hnT], L)
                    for j in range(ng):
                        c = g0 + j
                        TT(aT[:, c - sg0, 0:L], ft[j][:, 0:L], ub_[j][:, 0:L], ALU.mult, [ft[j], ub_[j]], [aT])
                    ps.free(ub_)

            def DN(sgi):
                sg0 = sgi * 8
                nsg = min(8, NFF - sg0)
                aT = actT[sgi % 2]
                for n in range(8):
                    bd = tm_proj(w_fd, 512 * n, 512, sg0, nsg, lambda k, b, aT=aT: aT[:, k, b * 128:(b + 1) * 128], [aT], nb)
                    for b in range(nb):
                        reg = Cblk[b].sub(512 * n, 512 * (n + 1))
                        TT(Cblk[b][:, 512 * n:512 * (n + 1)], bd[b][:, :], Cblk[b][:, 512 * n:512 * (n + 1)], ALU.add, [bd[b], reg], [reg])
                    ps.free(bd)

            nsgs = (NFF + 7) // 8
            GU(0)
            for sgi in range(1, nsgs):
                GU(sgi)
                DN(sgi - 1)
            DN(nsgs - 1)
            if dbg and "h2" in dbg and ltok0 == HALF:
                p.dma("sp", "dbg", [lambda e: e.dma_start(out=dbg_out["h2"], in_=Cblk[0].ap)], r=[Cblk[0]], w=[])

            chk(7)
            t0 = ltok0 - HALF
            p.dma("sp", "pld", [lambda e, t0=t0: e.dma_start(out=pbuf.ap, in_=pp[t0:t0 + L, :].rearrange("(b p) c -> p b c", p=128))],
                  r=[], w=[pbuf])
            for kc_ in range(2):
                bk = ps.alloc()
                p.group("pe", [TR(bk[:, b * 128:(b + 1) * 128], pbuf[:, b, kc_ * 128:(kc_ + 1) * 128], identF_t[:, :]) for b in range(nb)],
                        r=[pbuf, identF], w=[bk])
                ACT(pT[:, kc_, 0:L], bk[:, 0:L], AF.Copy, [bk], [pT])
                ps.free(bk)
            norm_T(nb, G_P)
            p.dma("sp", "gfin", [lambda e: e.dma_start(out=gfin.ap, in_=c_gfin)], r=[], w=[gfin])
            for n in range(8):
                bp = tm_proj(w_pp, 512 * n, 512, 0, 2, lambda k, b: pT[:, k, b * 128:(b + 1) * 128], [pT], nb)
                for b in range(nb):
                    ACT(ppo[b].ap, bp[b][:, :], AF.Copy, [bp[b]], [ppo[b]])
                ps.free(bp)
                bg = tm_proj(w_pg, 512 * n, 512, 0, 32, lambda k, b: hnT[:, k, b * 128:(b + 1) * 128], [hnT], nb)
                for b in range(nb):
                    s_ = sgt[b % 2]
                    ACT(s_.ap, bg[b][:, :], AF.Sigmoid, [bg[b]], [s_])
                    TT(s_.ap, s_.ap, ppo[b].ap, ALU.mult, [s_, ppo[b]], [s_])
                    reg = Cblk[b].sub(512 * n, 512 * (n + 1))
                    TT(Cblk[b][:, 512 * n:512 * (n + 1)], s_.ap, Cblk[b][:, 512 * n:512 * (n + 1)], ALU.add, [s_, reg], [reg])
                ps.free(bg)

            chk(8)
            for b in range(nb):
                ACT(junk.ap, Cblk[b].ap, AF.Square, [Cblk[b]], [junk, stat.sub(b, b + 1)], accum=stat_t[:, b:b + 1])
                ACT(stat_t[:, 8 + b:9 + b], stat_t[:, b:b + 1], AF.Sqrt, [stat.sub(b, b + 1), small], [stat.sub(8 + b, 9 + b)],
                    bias=eps_ap, scale=1.0 / D)
                RCP(stat_t[:, 8 + b:9 + b], stat_t[:, 8 + b:9 + b], [stat.sub(8 + b, 9 + b)], [stat.sub(8 + b, 9 + b)])
                STT(Cblk[b].ap, Cblk[b].ap, stat_t[:, 8 + b:9 + b], gfin.ap, ALU.mult, ALU.mult, [Cblk[b], stat.sub(8 + b, 9 + b), gfin], [Cblk[b]])
                p.dma("sp", f"ost{b}", [lambda e, t0=t0, b=b: e.dma_start(out=out[t0 + 128 * b:t0 + 128 * (b + 1), :], in_=Cblk[b].ap)],
                      r=[Cblk[b]], w=[("out", b, b + 1)])
          except _Stop:
            break
        final = [Tok(sem, p.dma_cnt[s]) for s, sem in p.dma_sems.items() if s.startswith("ost") or s == "dbg"]
        p.wait_only("sp", final)
        p.emit()
    return nc


def _host_consts(half, g_attn, g_ffn, g_ple, g_q, g_kv, conv_w, conv_b, g_final):
    pos = np.concatenate([np.arange(HALF), half * HALF + np.arange(HALF)]).astype(np.float64)
    if half == 0:
        pos[:HALF] = 0.0
    inv_r = 1.0 / (10000.0 ** (np.arange(0, 256, 2, dtype=np.float64) / 256))
    ang_r = inv_r[:, None] * pos[None, :]
    inv_m = 1.0 / (10000.0 ** (np.arange(0, 64, 2, dtype=np.float64) / 64))
    ang_m = inv_m[:, None] * pos[None, :]
    cosm = np.tile(np.cos(ang_m), (4, 1))
    sgn = np.where((np.arange(128) % 64) < 32, -1.0, 1.0)[:, None]
    sinm = np.tile(np.sin(ang_m), (4, 1)) * sgn
    c_cs = np.stack([np.cos(ang_r), np.sin(ang_r), cosm, sinm], axis=1).astype(np.float32)

    logg = np.log1p(-np.exp2(-5.0 - np.arange(8, dtype=np.float64)))
    i = np.arange(128)[:, None].astype(np.float64)
    n = np.arange(512)[None, :].astype(np.float64)
    ci, cn = (i // 64), (n // 64)
    dmqd = np.zeros((8, 128, 2, 512), np.float64)
    for h in range(8):
        same = np.exp(logg[h] * np.abs(n - i))
        later = np.exp(logg[h] * (n - i))
        dm = np.where(cn == ci, same, np.where(cn > ci, later, 0.0)) / 16.0
        dmqd[h, :, 0, :] = dm
        dmqd[h, :, 1, :] = np.exp(logg[h] * (n + 1.0))
    small = np.zeros((128, 1024), np.float64)
    fm = lambda v: np.asarray(v, np.float64).reshape(-1, 128).T
    small[:, 0:32] = fm(g_attn); small[:, 32:64] = fm(g_ffn); small[:, 64:96] = fm(g_ple)
    small[:, 96:104] = fm(g_q); small[:, 104:108] = fm(g_kv)
    small[:, 108:140] = 0.0
    if half == 0:
        small[:, 108:108 + 16] = NEG
    pidx = np.arange(128, dtype=np.float64)
    for li, L in enumerate((512, 384, 128)):
        for h in range(8):
            for b in range(4):
                small[:, 140 + li * 32 + h * 4 + b] = np.exp(logg[h] * (L - 1 - 128 * b - pidx)) / 16.0
    cw = np.asarray(conv_w, np.float64)
    for j in range(3):
        small[:, 236 + j:236 + 3 * NFF:3] = fm(cw[j])
    small[:, 494:494 + NFF] = fm(conv_b)
    small[:, 580] = EPS
    gfin = np.broadcast_to(np.asarray(g_final, np.float32)[None, :], (128, D)).copy()
    return c_cs, dmqd.astype(np.float32), small.astype(np.float32), gfin


_CACHE = {}


def _core_inputs(c, x, pfull, weights, consts):
    b, half = c // 2, c % 2
    xo = np.ascontiguousarray(x[b, half * HALF:(half + 1) * HALF])
    xp = np.ascontiguousarray(x[b, 0:HALF]) if half == 1 else np.zeros((HALF, D), np.float32)
    m = {"xo": xo, "xp": xp, "pp": np.ascontiguousarray(pfull[0, b, half * HALF:(half + 1) * HALF])}
    m.update(weights)
    c_cs, dmqd, small, gfin = consts[half]
    m.update({"c_cs": c_cs, "c_dmqd": dmqd, "c_small": small, "c_gfin": gfin, "c_ident": np.eye(128, dtype=np.float32)})
    return m


def kernel(x, p, w_in, g_attn, g_q_lora, g_kv_lora, w_uq, w_ukv, w_o, g_ffn, w_ffn_gate, w_ffn_up, conv_w, conv_b,
           w_ffn_down, g_ple, w_ple_gate, w_ple_proj, g_final):
    f = lambda a: np.ascontiguousarray(np.asarray(a, dtype=np.float32))
    x = f(x); p = f(p)
    weights = {"w_in": f(w_in)[0], "w_uq": f(w_uq)[0], "w_ukv": f(w_ukv)[0], "w_o": f(w_o)[0], "w_fg": f(w_ffn_gate)[0],
               "w_fu": f(w_ffn_up)[0], "w_fd": f(w_ffn_down)[0], "w_pg": f(w_ple_gate)[0], "w_pp": f(w_ple_proj)[0]}
    consts = [_host_consts(h, f(g_attn)[0], f(g_ffn)[0], f(g_ple)[0], f(g_q_lora)[0], f(g_kv_lora)[0], f(conv_w)[0], f(conv_b)[0],
                           f(g_final)) for h in range(2)]
    if "nc" not in _CACHE:
        _CACHE["nc"] = build_program()
    nc = _CACHE["nc"]
    in_maps = [_core_inputs(c, x, p, weights, consts) for c in range(8)]
    res = run_bass_kernel_spmd(nc, in_maps, core_ids=list(range(8)))
    outp = np.empty((4, SEQ, D), np.float32)
    for c in range(8):
        b, half = c // 2, c % 2
        outp[b, half * HALF:(half + 1) * HALF] = np.asarray(res.results[c]["out"], dtype=np.float32)
    return outp
```
